# Optimizing a Trainium2 kernel written in Bass

```python
import jax, jax.numpy as jnp
from jax import lax
import numpy as np

D_MODEL = 2048
BATCH = 4
SEQ = 4096
DEPTH = 1

D_SSM = D_MODEL
SSM_HEADDIM = 64
SSM_HEADS = D_SSM // SSM_HEADDIM
SSM_GROUPS = 8
SSM_STATE = 128
SSM_CONV = 4
CHUNK = 128
DT_MIN = 1e-3
DT_MAX = 1e-1
D_CONV = D_MODEL
SHORT_CONV = 3
D_MIX = D_SSM + D_CONV
D_FF = -(-(8 * D_MODEL) // (3 * 256)) * 256
EPS = 1e-5

D_XBC = D_SSM + 2 * SSM_GROUPS * SSM_STATE
OFF_Z = 0
OFF_XBC = OFF_Z + D_SSM
OFF_DT = OFF_XBC + D_XBC
OFF_CB = OFF_DT + SSM_HEADS
OFF_CC = OFF_CB + D_CONV
OFF_CX = OFF_CC + D_CONV
D_IN = OFF_CX + D_CONV

kernel_name = "hymba_ssd_shortconv_block"


def _rmsnorm(x, g):
    xf = x.astype(jnp.float32)
    y = xf * lax.rsqrt(jnp.mean(xf * xf, axis=-1, keepdims=True) + EPS)
    return (y * g.astype(jnp.float32)).astype(x.dtype)


def _causal_dwconv(u, w):
    K = w.shape[0]
    S = u.shape[1]
    up = jnp.pad(u, ((0, 0), (K - 1, 0), (0, 0)))
    y = up[:, K - 1:K - 1 + S] * w[K - 1]
    for k in range(K - 1):
        y = y + up[:, k:k + S] * w[k]
    return y


def _ssd_chunked(xh, dt, A, Bm, Cm):
    b, S, H, P = xh.shape
    G, N = Bm.shape[2], Bm.shape[3]
    R = H // G
    nc = S // CHUNK
    f32 = jnp.float32
    X = (xh.astype(f32) * dt[..., None]).reshape(b, nc, CHUNK, G, R, P)
    dA = jnp.moveaxis((dt * A).reshape(b, nc, CHUNK, G, R), 2, -1)
    Bc = Bm.astype(f32).reshape(b, nc, CHUNK, G, N)
    Cc = Cm.astype(f32).reshape(b, nc, CHUNK, G, N)
    dA_cs = jnp.cumsum(dA, axis=-1)
    causal = jnp.tril(jnp.ones((CHUNK, CHUNK), dtype=bool))
    seg = dA_cs[..., :, None] - dA_cs[..., None, :]
    L = jnp.exp(jnp.where(causal, seg, -jnp.inf))
    CB = jnp.einsum('bclgn,bcsgn->bcgls', Cc, Bc)
    M = CB[:, :, :, None] * L
    y_diag = jnp.einsum('bcgrls,bcsgrp->bclgrp', M, X)
    decay_states = jnp.exp(dA_cs[..., -1:] - dA_cs)
    states = jnp.einsum('bclgn,bcgrl,bclgrp->bcgrpn', Bc, decay_states, X)
    chunk_decay = jnp.exp(dA_cs[..., -1])

    def step(h, inp):
        dec, st = inp
        return h * dec[..., None, None] + st, h

    h0 = jnp.zeros((b, G, R, P, N), f32)
    _, prev = lax.scan(step, h0, (jnp.moveaxis(chunk_decay, 1, 0), jnp.moveaxis(states, 1, 0)))
    prev = jnp.moveaxis(prev, 0, 1)
    y_off = jnp.einsum('bclgn,bcgrpn,bcgrl->bclgrp', Cc, prev, jnp.exp(dA_cs))
    return (y_diag + y_off).reshape(b, S, H, P)


def _ssd_group(z, xbc, dt_raw, conv_w, conv_b, dt_bias, A_log, Dskip, norm_g):
    b, S, _ = z.shape
    xbc = jax.nn.silu(_causal_dwconv(xbc, conv_w) + conv_b)
    xs = xbc[..., :D_SSM].reshape(b, S, SSM_HEADS, SSM_HEADDIM)
    Bm = xbc[..., D_SSM:D_SSM + SSM_GROUPS * SSM_STATE].reshape(b, S, SSM_GROUPS, SSM_STATE)
    Cm = xbc[..., D_SSM + SSM_GROUPS * SSM_STATE:].reshape(b, S, SSM_GROUPS, SSM_STATE)
    dt = jax.nn.softplus(dt_raw.astype(jnp.float32) + dt_bias.astype(jnp.float32))
    A = -jnp.exp(A_log.astype(jnp.float32))
    y = _ssd_chunked(xs, dt, A, Bm, Cm)
    y = y + Dskip.astype(jnp.float32)[:, None] * xs.astype(jnp.float32)
    y = y.reshape(b, S, D_SSM).astype(z.dtype)
    return _rmsnorm(y * jax.nn.silu(z), norm_g)


def _shortconv_group(gb, gc, u, conv_w):
    return gb * _causal_dwconv(gc * u, conv_w)


def setup_inputs(seed: int = 0) -> dict:
    key = jax.random.key(seed)
    ks = jax.random.split(key, 16)
    f32 = jnp.float32
    nrm = lambda k, shape, s: jax.random.normal(k, shape, f32) * s
    x = jax.random.normal(ks[0], (BATCH, SEQ, D_MODEL), f32)
    norm_mix_g = 1.0 + nrm(ks[1], (DEPTH, D_MODEL), 0.02)
    w_in = nrm(ks[2], (DEPTH, D_MODEL, D_IN), D_MODEL ** -0.5)
    ssm_conv_w = nrm(ks[3], (DEPTH, SSM_CONV, D_XBC), SSM_CONV ** -0.5)
    ssm_conv_b = nrm(ks[4], (DEPTH, D_XBC), 0.02)
    dt0 = jnp.exp(jax.random.uniform(ks[5], (DEPTH, SSM_HEADS), f32)
                  * (np.log(DT_MAX) - np.log(DT_MIN)) + np.log(DT_MIN))
    ssm_dt_bias = dt0 + jnp.log(-jnp.expm1(-dt0))
    ssm_A_log = jnp.log(jax.random.uniform(ks[6], (DEPTH, SSM_HEADS), f32, 1.0, 16.0))
    ssm_D = 1.0 + nrm(ks[7], (DEPTH, SSM_HEADS), 0.1)
    ssm_norm_g = 1.0 + nrm(ks[8], (DEPTH, D_SSM), 0.02)
    sc_conv_w = nrm(ks[9], (DEPTH, SHORT_CONV, D_CONV), SHORT_CONV ** -0.5)
    w_out = nrm(ks[10], (DEPTH, D_MIX, D_MODEL), D_MIX ** -0.5)
    norm_ffn_g = 1.0 + nrm(ks[11], (DEPTH, D_MODEL), 0.02)
    w_gate = nrm(ks[12], (DEPTH, D_MODEL, D_FF), D_MODEL ** -0.5)
    w_up = nrm(ks[13], (DEPTH, D_MODEL, D_FF), D_MODEL ** -0.5)
    w_down = nrm(ks[14], (DEPTH, D_FF, D_MODEL), D_FF ** -0.5)
    norm_final_g = 1.0 + nrm(ks[15], (D_MODEL,), 0.02)
    return {"x": x, "norm_mix_g": norm_mix_g, "w_in": w_in, "ssm_conv_w": ssm_conv_w,
            "ssm_conv_b": ssm_conv_b, "ssm_dt_bias": ssm_dt_bias, "ssm_A_log": ssm_A_log,
            "ssm_D": ssm_D, "ssm_norm_g": ssm_norm_g, "sc_conv_w": sc_conv_w, "w_out": w_out,
            "norm_ffn_g": norm_ffn_g, "w_gate": w_gate, "w_up": w_up, "w_down": w_down,
            "norm_final_g": norm_final_g}


def reference(x, norm_mix_g, w_in, ssm_conv_w, ssm_conv_b, ssm_dt_bias, ssm_A_log, ssm_D,
              ssm_norm_g, sc_conv_w, w_out, norm_ffn_g, w_gate, w_up, w_down, norm_final_g):
    h = x
    for l in range(DEPTH):
        n = _rmsnorm(h, norm_mix_g[l])
        proj = jnp.einsum('bsd,de->bse', n, w_in[l])
        y_ssm = _ssd_group(proj[..., OFF_Z:OFF_XBC], proj[..., OFF_XBC:OFF_DT],
                           proj[..., OFF_DT:OFF_CB], ssm_conv_w[l], ssm_conv_b[l],
                           ssm_dt_bias[l], ssm_A_log[l], ssm_D[l], ssm_norm_g[l])
        y_sc = _shortconv_group(proj[..., OFF_CB:OFF_CC], proj[..., OFF_CC:OFF_CX],
                                proj[..., OFF_CX:D_IN], sc_conv_w[l])
        y_mix = jnp.concatenate([y_ssm, y_sc], axis=-1)
        h = h + jnp.einsum('bse,ed->bsd', y_mix, w_out[l])
        n2 = _rmsnorm(h, norm_ffn_g[l])
        g = jnp.einsum('bsd,df->bsf', n2, w_gate[l])
        u = jnp.einsum('bsd,df->bsf', n2, w_up[l])
        h = h + jnp.einsum('bsf,fd->bsd', jax.nn.silu(g) * u, w_down[l])
    return _rmsnorm(h, norm_final_g)
```

```python
import numpy as np
import concourse.bass as bass
import concourse.mybir as mybir
from concourse.bass_utils import run_bass_kernel_spmd

F32 = mybir.dt.float32
BF16 = mybir.dt.bfloat16
AF = mybir.ActivationFunctionType
ALU = mybir.AluOpType
AX = mybir.AxisListType

P = 128
D = 2048
KD = 16
DFF = 5632
KF = 44
NH = 32
HD = 64
NG = 8
DIN = 12320
OFF_Z, OFF_X, OFF_B, OFF_C, OFF_DT, OFF_CB, OFF_CC, OFF_CX = 0, 2048, 4096, 5120, 6144, 6176, 8224, 10272
TOK = 2048
TB = 512
NT = 4
NBLK = TOK // TB
EPS = 1e-5
NW = 3

C_MGT, C_MLE, C_ONE = 0, 128, 256
C_GMIX, C_GFFN, C_GSSM = 384, 400, 416
C_CW, C_CB, C_SCW = 432, 560, 592
C_DTB, C_ALOG, C_DSK, C_FLAG, C_ID = 640, 672, 704, 736, 737
C_DCH = 865
NCST = 881

SB_BASE = 16512
SB_END = 229344


class Op:
    __slots__ = ("eng", "emit", "deps", "tok", "signal", "num", "is_dma")


class Sched:
    ENG = ["pe", "act", "dve", "pool", "sp"]

    def __init__(self):
        self.ops = {e: [] for e in self.ENG}
        self.writers = {}
        self.readers = {}
        self.slots = {}
        self.all_dma = []

    def add(self, eng, emit, reads=(), writes=(), slot=None, extra=()):
        op = Op()
        op.eng, op.emit, op.signal, op.num = eng, emit, False, 0
        op.is_dma = slot is not None
        op.tok = None
        deps = set(extra)
        reads, writes = list(reads), list(writes)
        for r in list(reads):
            if isinstance(r, tuple) and r[0] == "ps":
                reads.remove(r)
                if r not in writes:
                    writes.append(r)
        for r in reads:
            for w in self.writers.get(r, {}).values():
                deps.add(w)
        for r in writes:
            isps = isinstance(r, tuple) and r[0] == "ps"
            for w in self.writers.get(r, {}).values():
                if w.is_dma or op.is_dma or w.eng != eng or (eng != "pe" and not isps):
                    deps.add(w)
            for rd in self.readers.get(r, {}).values():
                if rd.is_dma or op.is_dma or rd.eng != eng or (eng != "pe" and not isps):
                    deps.add(rd)
        for r in writes:
            self.readers[r] = {}
            self.writers.setdefault(r, {})[("dma", slot) if op.is_dma else eng] = op
        for r in reads:
            self.readers.setdefault(r, {})[("dma", slot) if op.is_dma else eng] = op
        for d in deps:
            if not d.is_dma:
                d.signal = True
        op.deps = deps
        if op.is_dma:
            n = self.slots.get(slot, 0) + 1
            self.slots[slot] = n
            op.tok = (slot, 16 * n)
            self.all_dma.append(op)
        self.ops[eng].append(op)
        return op

    def barrier(self, engs=("pe", "act", "dve", "sp")):
        lasts = []
        for e in engs:
            real = [o for o in self.ops[e] if o.emit is not None and not o.is_dma]
            if real:
                lasts.append(real[-1])
        dmas = list(self.all_dma)
        self.all_dma = []
        for e in engs:
            deps = [o for o in lasts if o.eng != e] + dmas
            self.add(e, None, extra=deps)

    def emit_all(self, nc, block, sems, dsems):
        for e in self.ENG:
            n = 0
            for op in self.ops[e]:
                if op.signal:
                    n += 1
                    op.num = n
        reg = {"pe": block.tensor, "act": block.scalar, "dve": block.vector,
               "pool": block.gpsimd, "sp": block.sync}
        for e in self.ENG:
            ops = self.ops[e]
            if not ops:
                continue

            def body(eng, e=e, ops=ops):
                waited = {}
                for op in ops:
                    for d in op.deps:
                        if d.is_dma:
                            key, val, sem = ("d", d.tok[0]), d.tok[1], dsems[d.tok[0]]
                        else:
                            key, val, sem = ("e", d.eng), d.num, sems[d.eng]
                        if waited.get(key, 0) < val:
                            waited[key] = val
                            eng.wait_ge(sem, val)
                    if op.emit is None:
                        continue
                    ins = op.emit(eng)
                    if op.is_dma:
                        ins.then_inc(dsems[op.tok[0]], 16)
                    elif op.signal:
                        ins.then_inc(sems[e], 1)

            reg[e](body)


def build_program(n_pro=NBLK, n_main=NBLK, stop=99, first_halo=True):
    nc = bass.Bass("TRN2", target_bir_lowering=False)
    xm = nc.dram_tensor("xm", [TOK, D], F32, kind="ExternalInput").ap()
    xp = nc.dram_tensor("xp", [TOK, D], F32, kind="ExternalInput").ap()
    w_in = nc.dram_tensor("w_in", [D, DIN], F32, kind="ExternalInput").ap()
    w_out = nc.dram_tensor("w_out", [2 * D, D], F32, kind="ExternalInput").ap()
    w_gate = nc.dram_tensor("w_gate", [D, DFF], F32, kind="ExternalInput").ap()
    w_up = nc.dram_tensor("w_up", [D, DFF], F32, kind="ExternalInput").ap()
    w_down = nc.dram_tensor("w_down", [DFF, D], F32, kind="ExternalInput").ap()
    cst_d = nc.dram_tensor("cst", [P, NCST], F32, kind="ExternalInput").ap()
    fg_d = nc.dram_tensor("fg", [1, D], F32, kind="ExternalInput").ap()
    out = nc.dram_tensor("out", [TOK, D], F32, kind="ExternalOutput").ap()

    S = Sched()
    off = [SB_BASE]

    def alloc(name, shape, dt, at=None):
        nb = int(np.prod(shape[1:])) * (4 if dt == F32 else 2)
        nb = (nb + 31) // 32 * 32
        if at is None:
            o = off[0]
            off[0] += nb
        else:
            o = at
        assert o + nb <= SB_END, (name, o, nb)
        return nc.alloc_sbuf_tensor_at(name, shape, dt, offset=o), o + nb

    def A(name, shape, dt):
        return alloc(name, shape, dt)[0]

    cst = A("cst", [P, NCST], F32)
    identb = A("identb", [P, P], BF16)
    a_bc = A("a_bc", [P, NH], F32)
    nT = A("nT", [P, KD, TB], BF16)
    nTh = A("nTh", [P, KD, 4], BF16)
    wbuf = [A(f"wbuf{i}", [P, KD, 512], BF16) for i in range(NW)]
    wdt = A("wdt", [P, KD, 32], BF16)
    hst = A("hst", [P, D], F32)
    hbf = A("hbf", [P, D], BF16)
    hist = A("hist", [P, 32, 3], F32)
    hist_sc = A("hist_sc", [P, 16, 2], F32)
    ss = A("ss", [P, 16], F32)
    rs = A("rs", [P, 16], F32)
    ssy = A("ssy", [P, 4], F32)
    dtx = [A(f"dtx{t}", [P, NH], F32) for t in range(NT)]
    dtl = [A(f"dtl{t}", [P, NH], F32) for t in range(NT)]
    dtv = [A(f"dtv{t}", [P, NH], F32) for t in range(NT)]
    dAv = [A(f"dAv{t}", [P, NH], F32) for t in range(NT)]
    dec3 = [A(f"dec3{t}", [P, 96], F32) for t in range(NT)]
    fac = [A(f"fac{t}", [P, NH], F32) for t in range(NT)]
    bdec = A("bdec", [P, NH], F32)
    diagD = A("diagD", [P, KD, P], BF16)
    AR = off[0]
    ymixT = A("ymixT", [P, 32, TB], BF16)
    scr8k = A("scr8k", [P, D], F32)
    xnb = [A(f"xn{i}", [P, D], BF16) for i in range(2)]
    sz = []
    SZ0_OFF = off[0]
    for t in range(NT):
        if t == 2:
            SZ2_OFF = off[0]
        sz.append(A(f"sz{t}", [P, D], BF16))
    xt2, _ = alloc("xt2", [P, D], F32, at=SZ0_OFF)
    fgbc, _ = alloc("fgbc", [P, D], F32, at=SZ2_OFF)
    AR_REST = off[0]
    BT = A("BT", [P, NG, TB], BF16)
    CT = A("CT", [P, NG, TB], BF16)
    Xb = A("Xb", [P, D], BF16)
    Xd = A("Xd", [P, D], BF16)
    Btm = A("Btm", [P, NG * P], BF16)
    Mb = [A(f"Mb{i}", [P, 512], BF16) for i in range(8)]
    Lb = [A(f"Lb{i}", [P, 512], BF16) for i in range(2)]
    cbm = [A(f"cbm{i}", [P, P], BF16) for i in range(2)]
    dat = [A(f"dat{i}", [P, 512], F32) for i in range(2)]
    NF2K = 5
    f2k = [A(f"f2k{i}", [P, 516], F32) for i in range(NF2K)]
    t1 = A("t1", [P, 512], F32)
    t3 = A("t3", [P, 512], BF16)
    MIX_END = off[0]
    o = AR_REST
    hres = []
    for t in range(NT):
        h_, o = alloc(f"hres{t}", [P, D], F32, at=o)
        hres.append(h_)
    actB, o = alloc("actB", [P, KF - 32, TB], BF16, at=o)
    actA, _ = alloc("actA", [P, 32, TB], BF16, at=AR)
    sg = []
    for i in range(4):
        s_, o = alloc(f"sg{i}", [P, 512], F32, at=o)
        sg.append(s_)
    assert max(o, MIX_END) <= SB_END, (o, MIX_END)
    print("SBUF end: mixer", MIX_END, "ffn", o, "limit", SB_END)

    def actT(k):
        return actA[:, k, :] if k < 32 else actB[:, k - 32, :]

    ps = [nc.alloc_psum_tensor(f"ps{i}", [P, 512], F32) for i in range(8)]
    psb = [p_[:].bitcast(BF16) for p_ in ps]
    rot = {"A": 0, "M": 0, "f2k": 0, "w": 0, "Mb": 0, "Lb": 0, "cbm": 0, "dat": 0, "os": 0, "xn": 0}

    pools = {"A": [0, 1, 2, 3], "M": [4, 5, 6, 7]}

    def set_pools(na):
        pools["A"] = list(range(na))
        pools["M"] = list(range(na, 8))

    def bankA():
        rot["A"] = (rot["A"] + 1) % len(pools["A"])
        return pools["A"][rot["A"]]

    def bankM():
        rot["M"] = (rot["M"] + 1) % len(pools["M"])
        return pools["M"][rot["M"]]

    def nxt(name, n):
        rot[name] = (rot[name] + 1) % n
        return rot[name]

    def c1(col, n=1):
        return cst[:, col:col + n]

    def mm(out_, lhsT, rhs, start, stop, reads, writes):
        S.add("pe", lambda e: e.matmul(out_, lhsT, rhs, start=start, stop=stop), reads, writes)

    def tr(out_, in_, reads, writes):
        S.add("pe", lambda e: e.transpose(out_, in_, identb[:]), list(reads) + ["identb"], writes)

    def act(out_, in_, func, reads, writes, **kw):
        S.add("act", lambda e: e.activation(out=out_, in_=in_, func=func, **kw), reads, writes)

    def vtt(out_, in0, in1, op, reads, writes):
        S.add("dve", lambda e: e.tensor_tensor(out=out_, in0=in0, in1=in1, op=op), reads, writes)

    def vstt(out_, in0, scalar, in1, op0, op1, reads, writes):
        S.add("dve", lambda e: e.scalar_tensor_tensor(out=out_, in0=in0, scalar=scalar, in1=in1,
                                                      op0=op0, op1=op1), reads, writes)

    def vsmul(out_, in0, scalar, reads, writes):
        S.add("dve", lambda e: e.tensor_scalar_mul(out=out_, in0=in0, scalar1=scalar), reads, writes)

    def vcopy(out_, in_, reads, writes):
        S.add("dve", lambda e: e.tensor_copy(out=out_, in_=in_), reads, writes)

    def dma(q, out_, in_, slot, reads, writes):
        return S.add(q, lambda e: e.dma_start(out=out_, in_=in_), reads, writes, slot=slot)

    def wload(src, nk, ncols):
        s = nxt("w", NW)
        dma("pool", wbuf[s][:, 0:nk, 0:ncols], src, f"w{s}", [], [("w", s)])
        return s

    def wsrc(w, r0, nk, c0, ncols):
        return w[r0:r0 + nk * P, c0:c0 + ncols].rearrange("(k p) c -> p k c", p=P)

    dma("sp", cst[:], cst_d, "cst", [], ["cst"])
    vcopy(identb[:], cst[:, C_ID:C_ID + P], ["cst"], ["identb"])
    act(a_bc[:], c1(C_ALOG, NH), AF.Exp, ["cst"], ["a_bc"])
    vsmul(a_bc[:], a_bc[:], -1.0, ["a_bc"], ["a_bc"])
    for k_ in range(KD):
        vsmul(diagD[:, k_, :], cst[:, C_ID:C_ID + P], c1(C_DCH + k_), ["cst"], ["diagD"])
    S.add("dve", lambda e: e.memset(hst[:], 0.0), [], [("hst", q_) for q_ in range(4)])
    S.add("dve", lambda e: e.memset(hbf[:], 0.0), [], [("hbf", q_) for q_ in range(4)])
    S.add("dve", lambda e: e.memset(hist[:], 0.0), [], [("hist", c_) for c_ in range(32)])
    S.add("dve", lambda e: e.memset(hist_sc[:], 0.0), [], [("hsc", c_) for c_ in range(16)])

    def rstd_ops(c):
        act(rs[:, c:c + 1], ss[:, c:c + 1], AF.Ln, [("ss", c)], [("rs", c)], scale=1.0 / D, bias=EPS)
        act(rs[:, c:c + 1], rs[:, c:c + 1], AF.Exp, [("rs", c)], [("rs", c)], scale=-0.5)

    NTALL = [("nT", t) for t in range(NT)]

    def norm_stage1(src, sres, c):
        xi = nxt("xn", 2)
        xbuf, xres = xnb[xi], [("xn", xi)]
        act(xbuf[:], src, AF.Square, sres, xres + [("ss", c)], accum_out=ss[:, c:c + 1])
        rstd_ops(c)
        vsmul(xbuf[:], src, rs[:, c:c + 1], sres + [("rs", c)], xres)
        return (xbuf, xres)

    def norm_to_nT(src_fn, src_res_fn, gcol, col0, pre=None):
        bufs = dict(pre or {})

        def stage1(t):
            if t in bufs:
                return
            bufs[t] = norm_stage1(src_fn(t), src_res_fn(t), col0 + t)

        def stage2(t):
            xbuf, xres = bufs[t]
            for half in range(2):
                b = bankM()
                for kk in range(8):
                    k = half * 8 + kk
                    tr(psb[b][:, kk * P:(kk + 1) * P], xbuf[:, k * P:(k + 1) * P], xres, [("ps", b)])
                k0 = half * 8
                vtt(nT[:, k0:k0 + 8, t * P:(t + 1) * P], psb[b][:, 0:1024].rearrange("p (k c) -> p k c", k=8),
                    cst[:, gcol + k0:gcol + k0 + 8].unsqueeze(2).to_broadcast([P, 8, P]), ALU.mult,
                    [("ps", b), "cst"], [("nT", t)])

        stage1(0)
        for t in range(NT):
            if t + 1 < NT:
                stage1(t + 1)
            stage2(t)

    def formW(s, e, nk=KD):
        b = bankA()
        for k in range(nk):
            mm(ps[b][:], wbuf[s][:, k, e * P:(e + 1) * P], nT[:, k, :], k == 0, k == nk - 1,
               [("w", s)] + NTALL, [("ps", b)])
        return b

    def halo_mm(s, e, ncol):
        b = bankM()
        for k in range(KD):
            mm(ps[b][:, 0:ncol], wbuf[s][:, k, e * P:(e + 1) * P], nTh[:, k, 3 - ncol:3], k == 0, k == KD - 1,
               [("w", s), "nTh"], [("ps", b)])
        return b

    def conv_front(ch, b, halo_b):
        ui = nxt("f2k", NF2K)
        ai = nxt("f2k", NF2K)
        ue, a = f2k[ui], f2k[ai]
        ur, ar = ("f2k", ui), ("f2k", ai)
        if halo_b is not None:
            act(hist[:, ch, :], ps[halo_b][:, 0:3], AF.Copy, [("ps", halo_b), "cst"], [("hist", ch)], scale=c1(C_FLAG))
        act(ue[:, 0:3], hist[:, ch, :], AF.Copy, [("hist", ch)], [ur])
        act(ue[:, 3:515], ps[b][:], AF.Copy, [("ps", b)], [ur])
        act(a[:, 0:512], ps[b][:], AF.Identity, [("ps", b), "cst"], [ar],
            scale=c1(C_CW + ch * 4 + 3), bias=c1(C_CB + ch))
        act(hist[:, ch, :], ue[:, 512:515], AF.Copy, [ur], [("hist", ch)])
        for tap in (2, 1, 0):
            vstt(a[:, 0:512], ue[:, tap:tap + 512], c1(C_CW + ch * 4 + tap), a[:, 0:512], ALU.mult, ALU.add,
                 [ur, ar, "cst"], [ar])
        return ai

    def conv_back(dst, dres, ai):
        act(dst, f2k[ai][:, 0:512], AF.Silu, [("f2k", ai)], dres)

    yn_idx = {}
    FGK = [("sz", t_, j_) for t_ in (2, 3) for j_ in range(4)]
    XT2K = [("sz", t_, j_) for t_ in (0, 1) for j_ in range(4)]

    prefetched = set()
    hoisted = {}
    deferred = []

    def block(bi, xsrc, main, first_main, out_rows, barrier_after_z, nxt_blk):
        def xsrc_fn(t):
            buf, keys = (scr8k, ["scr8k"]) if t % 2 == 0 else (xt2, XT2K)
            if (bi, main, t) not in prefetched:
                dma("sp", buf[:], xsrc[(bi * NT + t) * P:(bi * NT + t + 1) * P, :], f"xt{t % 2}", [], keys)
            return buf[:]

        def prefetch_next():
            if nxt_blk is None:
                return
            nbi, nmain, nsrc = nxt_blk
            for t in range(2):
                buf, keys = (scr8k, ["scr8k"]) if t % 2 == 0 else (xt2, XT2K)
                dma("sp", buf[:], nsrc[(nbi * NT + t) * P:(nbi * NT + t + 1) * P, :], f"xt{t % 2}", [], keys)
                prefetched.add((nbi, nmain, t))

        def hoist_next_norm():
            if nxt_blk is None:
                return
            nbi, nmain, nsrc = nxt_blk
            d_ = {}
            for t in range(2):
                buf, keys = (scr8k, ["scr8k"]) if t % 2 == 0 else (xt2, XT2K)
                d_[t] = norm_stage1(buf[:], keys, t)
            hoisted[(nbi, nmain)] = d_

        norm_to_nT(xsrc_fn, lambda t: ["scr8k"] if t % 2 == 0 else XT2K, C_GMIX, 0, hoisted.pop((bi, main), None))
        while deferred:
            deferred.pop(0)()

        dma("pool", wdt[:], wsrc(w_in, 0, KD, OFF_DT, 32), "wdt", [], ["wdt"])
        set_pools(3)

        def dt_part1():
            dtb_ = []
            for t in range(NT):
                b = bankM()
                for k in range(KD):
                    mm(ps[b][:, 0:NH], nT[:, k, t * P:(t + 1) * P], wdt[:, k, :], k == 0, k == KD - 1,
                       ["wdt", ("nT", t)], [("ps", b)])
                dtb_.append(b)
            for t in range(NT):
                b = dtb_[t]
                vtt(dtx[t][:], ps[b][:, 0:NH], c1(C_DTB, NH), ALU.add, [("ps", b), "cst"], [("dtx", t)])
                vstt(dtl[t][:], dtx[t][:], -1.0, dtx[t][:], ALU.mult, ALU.max, [("dtx", t)], [("dtl", t)])
                act(dtl[t][:], dtl[t][:], AF.Exp, [("dtl", t)], [("dtl", t)], scale=-1.0)
                act(dtl[t][:], dtl[t][:], AF.Ln, [("dtl", t)], [("dtl", t)], bias=1.0)
                vstt(dtv[t][:], dtx[t][:], 0.0, dtl[t][:], ALU.max, ALU.add, [("dtx", t), ("dtl", t)], [("dtv", t)])
                vtt(dAv[t][:], dtv[t][:], a_bc[:], ALU.mult, [("dtv", t), "a_bc"], [("dAv", t)])

        if not main:
            dt_part1()

        def dt_part2():
            for t in range(NT):
                b2 = bankM()
                for i, mcol in enumerate((C_MGT, C_MLE, C_ONE)):
                    mm(ps[b2][:, i * NH:(i + 1) * NH], cst[:, mcol:mcol + P], dAv[t][:], True, True,
                       ["cst", ("dAv", t)], [("ps", b2)])
                act(dec3[t][:], ps[b2][:, 0:96], AF.Exp, [("ps", b2)], [("dec3", t)])

        if main:
            for j in range(4):
                s = wload(wsrc(w_in, 0, KD, OFF_Z + j * 512, 512), KD, 512)
                for t in range(NT):
                    b = bankA()
                    for k in range(KD):
                        mm(ps[b][:], nT[:, k, t * P:(t + 1) * P], wbuf[s][:, k, :], k == 0, k == KD - 1,
                           [("w", s), ("nT", t)], [("ps", b)])
                    act(sz[t][:, j * 512:(j + 1) * 512], ps[b][:], AF.Silu, [("ps", b)], [("sz", t, j)])
                if j == 0:
                    dt_part1()
            dt_part2()
        if barrier_after_z:
            S.barrier()
        if main and stop <= 1:
            return
        secs = [(OFF_X, 4, 0), (OFF_B, 2, 16)] + ([(OFF_C, 2, 24)] if main else [])
        pending = None
        for c0, nb, ch0 in secs:
            for jb in range(nb):
                s = wload(wsrc(w_in, 0, KD, c0 + jb * 512, 512), KD, 512)
                for e in range(4):
                    ch = ch0 + jb * 4 + e
                    b = formW(s, e)
                    hb = halo_mm(s, e, 3) if first_main else None
                    if ch < 16:
                        dst, dres = ymixT[:, ch, :], [("ymT", ch, tt) for tt in range(NT)]
                    elif ch < 24:
                        dst, dres = BT[:, ch - 16, :], [("BT", ch - 16)]
                    else:
                        dst, dres = CT[:, ch - 24, :], [("CT", ch - 24)]
                    ai = conv_front(ch, b, hb)
                    if pending is not None:
                        conv_back(*pending)
                    pending = (dst, dres, ai)
                if (not main) and c0 == OFF_X and jb == 0:
                    dt_part2()
        conv_back(*pending)
        if main and stop <= 2:
            return

        def ssd1_pre(t):
            tc = slice(t * P, (t + 1) * P)
            for half in range(2):
                b = bankM()
                for kk in range(8):
                    k = half * 8 + kk
                    tr(psb[b][:, kk * P:(kk + 1) * P], ymixT[:, k, tc], [("ymT", k, t)], [("ps", b)])
                hs = slice(half * 1024, (half + 1) * 1024)
                dtb = dtv[t][:, half * 16:(half + 1) * 16].unsqueeze(2).to_broadcast([P, 16, HD])
                vtt(Xb[:, hs].rearrange("p (h d) -> p h d", h=16),
                    psb[b][:, 0:1024].rearrange("p (h d) -> p h d", h=16), dtb, ALU.mult,
                    [("ps", b), ("dtv", t)], [("Xb", half)])
                vtt(Xd[:, hs].rearrange("p (h d) -> p h d", h=16),
                    Xb[:, hs].rearrange("p (h d) -> p h d", h=16),
                    dec3[t][:, half * 16:(half + 1) * 16].unsqueeze(2).to_broadcast([P, 16, HD]), ALU.mult,
                    [("Xb", half), ("dec3", t)], [("Xd", half)])
                yield
            b = bankM()
            for g in range(NG):
                tr(psb[b][:, g * P:(g + 1) * P], BT[:, g, tc], [("BT", g)], [("ps", b)])
            act(Btm[:], psb[b][:, 0:1024], AF.Copy, [("ps", b)], ["Btm"])
            yield

        def state_update(t):
            for q in range(4):
                qs = slice(q * 512, (q + 1) * 512)
                b = bankM()
                for g2 in range(2):
                    g = q * 2 + g2
                    mm(ps[b][:, g2 * 256:(g2 + 1) * 256], Btm[:, g * P:(g + 1) * P], Xd[:, g * 256:(g + 1) * 256],
                       True, True, ["Btm", ("Xd", q // 2)], [("ps", b)])
                vtt(hst[:, qs].rearrange("p (h d) -> p h d", h=8), hst[:, qs].rearrange("p (h d) -> p h d", h=8),
                    dec3[t][:, 64 + q * 8:64 + q * 8 + 8].unsqueeze(2).to_broadcast([P, 8, HD]), ALU.mult,
                    [("hst", q), ("dec3", t)], [("hst", q)])
                vtt(hst[:, qs], hst[:, qs], ps[b][:], ALU.add, [("hst", q), ("ps", b)], [("hst", q)])
                act(hbf[:, qs], hst[:, qs], AF.Copy, [("hst", q)], [("hbf", q)])
                yield

        def stageA(t):
            tc = slice(t * P, (t + 1) * P)

            def mk_dat(g):
                di = g % 2
                for hh in range(4):
                    act(dat[di][:, hh * P:(hh + 1) * P], cst[:, C_MLE:C_MLE + P], AF.Copy,
                        ["cst", ("dAv", t)], [("dat", di)], scale=dAv[t][:, 4 * g + hh:4 * g + hh + 1])

            def group(g):
                di = li = ci = g % 2
                b = bankM()
                mm(ps[b][:], cst[:, C_MGT:C_MGT + P], dat[di][:], True, True, ["cst", ("dat", di)], [("ps", b)])
                act(Lb[li][:], ps[b][:], AF.Exp, [("ps", b)], [("Lb", li)])
                b2 = bankM()
                mm(ps[b2][:, 0:P], BT[:, g, tc], CT[:, g, tc], True, True, [("BT", g), ("CT", g)], [("ps", b2)])
                if g + 2 < NG:
                    mk_dat(g + 2)
                vtt(cbm[ci][:], ps[b2][:, 0:P], cst[:, C_MLE:C_MLE + P], ALU.mult, [("ps", b2), "cst"], [("cbm", ci)])
                vtt(Mb[g][:].rearrange("p (h l) -> p h l", h=4),
                    Lb[li][:].rearrange("p (h l) -> p h l", h=4),
                    cbm[ci][:].unsqueeze(1).to_broadcast([P, 4, P]), ALU.mult,
                    [("Lb", li), ("cbm", ci)], [("Mb", g)])

            mk_dat(0)
            mk_dat(1)
            pre = ssd1_pre(t)
            for g in range(NG):
                group(g)
                yield
                if g % 2 == 1:
                    next(pre, None)
                    yield
            for _ in pre:
                yield

        def stageB(t):
            tc = slice(t * P, (t + 1) * P)
            for q in range(4):
                half = q // 2
                qs = slice(q * 512, (q + 1) * 512)
                bd = bankM()
                for j in range(4):
                    k = q * 4 + j
                    S.add("pe", lambda e, bd=bd, j=j, k=k: e.matmul(
                        ps[bd][:, j * P:(j + 1) * P], ymixT[:, k, tc], diagD[:, k, :], start=(j == 0), stop=False,
                        skip_group_check=True), [("ymT", k, t), "diagD"], [("ps", bd)])
                for hh in range(8):
                    h = q * 8 + hh
                    g = h // 4
                    S.add("pe", lambda e, bd=bd, hh=hh, h=h, g=g: e.matmul(
                        ps[bd][:, hh * HD:(hh + 1) * HD], Mb[g][:, (h % 4) * P:(h % 4 + 1) * P],
                        Xb[:, h * HD:(h + 1) * HD], start=False, stop=(hh == 7), skip_group_check=True),
                        [("Mb", g), ("Xb", half)], [("ps", bd)])
                bo = bankM()
                for g2 in range(2):
                    g = q * 2 + g2
                    mm(ps[bo][:, g2 * 256:(g2 + 1) * 256], CT[:, g, tc], hbf[:, g * 256:(g + 1) * 256], True, True,
                       [("CT", g), ("hbf", q)], [("ps", bo)])
                vtt(t1[:].rearrange("p (h d) -> p h d", h=8), ps[bo][:].rearrange("p (h d) -> p h d", h=8),
                    dec3[t][:, 32 + q * 8:32 + q * 8 + 8].unsqueeze(2).to_broadcast([P, 8, HD]), ALU.mult,
                    [("ps", bo), ("dec3", t)], ["t1"])
                vtt(t1[:], ps[bd][:], t1[:], ALU.add, [("ps", bd), "t1"], ["t1"])
                vtt(scr8k[:, qs], t1[:], sz[t][:, qs], ALU.mult, ["t1", ("sz", t, q)], ["scr8k"])
                act(t3[:], scr8k[:, qs], AF.Square, ["scr8k"], ["t3", ("ssy", q)], accum_out=ssy[:, q:q + 1])
                yield
            c = 8 + t
            S.add("dve", lambda e, c=c: e.reduce_sum(out=ss[:, c:c + 1], in_=ssy[:, 0:4], axis=AX.X),
                  [("ssy", q) for q in range(4)], [("ss", c)])
            rstd_ops(c)
            yi = nxt("xn", 2)
            yn_idx[t] = yi
            act(xnb[yi][:], scr8k[:], AF.Copy, ["scr8k", ("rs", c)], [("xn", yi)], scale=rs[:, c:c + 1])
            yield
            yield from state_update(t)

        def y_transposes(t):
            tc = slice(t * P, (t + 1) * P)
            yi = yn_idx[t]
            for half in range(2):
                b = bankM()
                for kk in range(8):
                    k = half * 8 + kk
                    tr(psb[b][:, kk * P:(kk + 1) * P], xnb[yi][:, k * P:(k + 1) * P], [("xn", yi)], [("ps", b)])
                k0 = half * 8
                vtt(ymixT[:, k0:k0 + 8, tc], psb[b][:, 0:1024].rearrange("p (k c) -> p k c", k=8),
                    cst[:, C_GSSM + k0:C_GSSM + k0 + 8].unsqueeze(2).to_broadcast([P, 8, P]), ALU.mult,
                    [("ps", b), "cst"], [("ymT", k0 + kk, t) for kk in range(8)])
                yield

        def sc_group(j):
            s = wload(wsrc(w_in, 0, KD, OFF_CC + j * 512, 512), KD, 512)
            ccs = []
            for e in range(4):
                b = formW(s, e)
                hb = halo_mm(s, e, 2) if first_main else None
                fi = nxt("f2k", NF2K)
                act(f2k[fi][:, 0:512], ps[b][:], AF.Copy, [("ps", b)], [("f2k", fi)])
                ccs.append(fi)
                if hb is not None:
                    act(hist_sc[:, j * 4 + e, :], ps[hb][:, 0:2], AF.Copy, [("ps", hb), "cst"], [("hsc", j * 4 + e)],
                        scale=c1(C_FLAG))
                yield
            s = wload(wsrc(w_in, 0, KD, OFF_CX + j * 512, 512), KD, 512)
            for e in range(4):
                ch = j * 4 + e
                b = formW(s, e)
                if first_main:
                    hb = halo_mm(s, e, 2)
                    vtt(hist_sc[:, ch, :], hist_sc[:, ch, :], ps[hb][:, 0:2], ALU.mult, [("hsc", ch), ("ps", hb)], [("hsc", ch)])
                gi = nxt("f2k", NF2K)
                while gi in ccs:
                    gi = nxt("f2k", NF2K)
                ge = f2k[gi]
                a = f2k[ccs[e]]
                vcopy(ge[:, 0:2], hist_sc[:, ch, :], [("hsc", ch)], [("f2k", gi)])
                vtt(ge[:, 2:514], ps[b][:], a[:, 0:512], ALU.mult, [("ps", b), ("f2k", ccs[e])], [("f2k", gi)])
                act(a[:, 0:512], ge[:, 2:514], AF.Copy, [("f2k", gi), "cst"], [("f2k", ccs[e])], scale=c1(C_SCW + ch * 3 + 2))
                vcopy(hist_sc[:, ch, :], ge[:, 512:514], [("f2k", gi)], [("hsc", ch)])
                for tap in (1, 0):
                    vstt(a[:, 0:512], ge[:, tap:tap + 512], c1(C_SCW + ch * 3 + tap), a[:, 0:512], ALU.mult, ALU.add,
                         [("f2k", gi), ("f2k", ccs[e]), "cst"], [("f2k", ccs[e])])
                yield
            s = wload(wsrc(w_in, 0, KD, OFF_CB + j * 512, 512), KD, 512)
            for e in range(4):
                ch = j * 4 + e
                b = formW(s, e)
                vtt(ymixT[:, 16 + ch, :], ps[b][:], f2k[ccs[e]][:, 0:512], ALU.mult, [("ps", b), ("f2k", ccs[e])],
                    [("ymT", 16 + ch, tt) for tt in range(NT)])
                yield

        def run(gen):
            for _ in gen:
                pass

        if not main:
            set_pools(4)
            accb = [bankA() for _ in range(4)]
            for t in range(NT):
                vtt(fac[t][:], dtv[t][:], dec3[t][:, 0:NH], ALU.mult, [("dtv", t), ("dec3", t)], [("fac", t)])
                for t2 in range(t + 1, NT):
                    vtt(fac[t][:], fac[t][:], dec3[t2][:, 64:96], ALU.mult, [("fac", t), ("dec3", t2)], [("fac", t)])
            vtt(bdec[:], dec3[0][:, 64:96], dec3[1][:, 64:96], ALU.mult, [("dec3", 0), ("dec3", 1)], ["bdec"])
            for t2 in (2, 3):
                vtt(bdec[:], bdec[:], dec3[t2][:, 64:96], ALU.mult, ["bdec", ("dec3", t2)], ["bdec"])
            for t in range(NT):
                tc = slice(t * P, (t + 1) * P)
                for half in range(2):
                    b = bankM()
                    for kk in range(8):
                        k = half * 8 + kk
                        tr(psb[b][:, kk * P:(kk + 1) * P], ymixT[:, k, tc], [("ymT", k, t)], [("ps", b)])
                    hs = slice(half * 1024, (half + 1) * 1024)
                    vtt(Xd[:, hs].rearrange("p (h d) -> p h d", h=16),
                        psb[b][:, 0:1024].rearrange("p (h d) -> p h d", h=16),
                        fac[t][:, half * 16:(half + 1) * 16].unsqueeze(2).to_broadcast([P, 16, HD]), ALU.mult,
                        [("ps", b), ("fac", t)], [("Xd", half)])
                b = bankM()
                for g in range(NG):
                    tr(psb[b][:, g * P:(g + 1) * P], BT[:, g, tc], [("BT", g)], [("ps", b)])
                act(Btm[:], psb[b][:, 0:1024], AF.Copy, [("ps", b)], ["Btm"])
                for q in range(4):
                    for g2 in range(2):
                        g = q * 2 + g2
                        S.add("pe", lambda e, q=q, g2=g2, g=g, t=t: e.matmul(
                            ps[accb[q]][:, g2 * 256:(g2 + 1) * 256], Btm[:, g * P:(g + 1) * P],
                            Xd[:, g * 256:(g + 1) * 256], start=(t == 0 and g2 == 0),
                            stop=(t == NT - 1 and g2 == 1), skip_group_check=True),
                            ["Btm", ("Xd", q // 2)], [("ps", accb[q])])
                if t == 0:
                    prefetch_next()
                if t == 2:
                    hoist_next_norm()
            for q in range(4):
                qs = slice(q * 512, (q + 1) * 512)
                vtt(hst[:, qs].rearrange("p (h d) -> p h d", h=8), hst[:, qs].rearrange("p (h d) -> p h d", h=8),
                    bdec[:, q * 8:q * 8 + 8].unsqueeze(2).to_broadcast([P, 8, HD]), ALU.mult,
                    [("hst", q), "bdec"], [("hst", q)])
                vtt(hst[:, qs], hst[:, qs], ps[accb[q]][:], ALU.add, [("hst", q), ("ps", accb[q])], [("hst", q)])
            return

        def micro_steps():
            yield from stageA(0)
            for t in range(NT):
                yield from stageB(t)
                if t + 1 < NT:
                    yield from stageA(t + 1)
                if t > 0:
                    yield from y_transposes(t - 1)
            yield from y_transposes(NT - 2) if False else iter(())

        def sc_units():
            for j in range(4):
                yield from sc_group(j)

        micro = micro_steps()
        MICRO_PER_UNIT = 2
        for _ in sc_units():
            for _i in range(MICRO_PER_UNIT):
                next(micro, None)
        run(micro)
        if stop <= 5:
            return

        def outproj_part(j, kh, banks, first, last):
            s = wload(wsrc(w_out, kh * D, KD, j * 512, 512), KD, 512)
            for t in range(NT):
                for k in range(KD):
                    kg = kh * KD + k
                    mm(ps[banks[t]][:], ymixT[:, kg, t * P:(t + 1) * P], wbuf[s][:, k, :],
                       first and k == 0, last and k == KD - 1, [("w", s), ("ymT", kg, t)], [("ps", banks[t])])

        set_pools(4)
        banks = [bankA() for _ in range(NT)]
        outproj_part(0, 1, banks, True, False)
        run(y_transposes(NT - 1))
        S.barrier()
        for t in range(NT):
            dma("sp", hres[t][:], xsrc[(bi * NT + t) * P:(bi * NT + t + 1) * P, :], f"xr{t}", [], [("hres", t)])
        for j in range(4):
            if j > 0:
                banks = [bankA() for _ in range(NT)]
                outproj_part(j, 1, banks, True, False)
            outproj_part(j, 0, banks, False, True)
            for t in range(NT):
                vtt(hres[t][:, j * 512:(j + 1) * 512], hres[t][:, j * 512:(j + 1) * 512], ps[banks[t]][:], ALU.add,
                    [("hres", t), ("ps", banks[t])], [("hres", t)])
        S.barrier()
        if stop <= 6:
            return

        prefetch_next()
        dma("sp", fgbc[:], fg_d.partition_broadcast(P), "fg", [], FGK)
        norm_to_nT(lambda t: hres[t][:], lambda t: [("hres", t)], C_GFFN, 4)
        for j in range(KF // 4):
            s = wload(wsrc(w_gate, 0, KD, j * 512, 512), KD, 512)
            for e in range(4):
                b = formW(s, e)
                act(sg[e][:], ps[b][:], AF.Silu, [("ps", b)], [("sg", e)])
            s = wload(wsrc(w_up, 0, KD, j * 512, 512), KD, 512)
            for e in range(4):
                b = formW(s, e)
                vtt(actT(j * 4 + e), ps[b][:], sg[e][:], ALU.mult, [("ps", b), ("sg", e)], [("actT", j * 4 + e)])
        if stop <= 7:
            return
        hoist_next_norm()
        for j in range(4):
            banks = [bankA() for _ in range(NT)]
            for (k0, nk) in ((0, 16), (16, 16), (32, 12)):
                s = wload(wsrc(w_down, k0 * P, nk, j * 512, 512), nk, 512)
                for t in range(NT):
                    for k in range(nk):
                        kg = k0 + k
                        mm(ps[banks[t]][:], actT(kg)[:, t * P:(t + 1) * P], wbuf[s][:, k, :], kg == 0, kg == KF - 1,
                           [("w", s), ("actT", kg)], [("ps", banks[t])])
            for t in range(NT):
                vtt(hres[t][:, j * 512:(j + 1) * 512], hres[t][:, j * 512:(j + 1) * 512], ps[banks[t]][:], ALU.add,
                    [("hres", t), ("ps", banks[t])], [("hres", t)])
        def final_norm():
            for t in range(NT):
                c = 12 + t
                xi = nxt("xn", 2)
                act(xnb[xi][:], hres[t][:], AF.Square, [("hres", t)], [("xn", xi), ("ss", c)], accum_out=ss[:, c:c + 1])
                rstd_ops(c)
                vstt(hres[t][:], hres[t][:], rs[:, c:c + 1], fgbc[:], ALU.mult, ALU.mult,
                     [("hres", t), ("rs", c)] + FGK, [("hres", t)])
                stores.append(dma("sp", out[(out_rows + t) * P:(out_rows + t + 1) * P, :], hres[t][:], f"st{t}",
                                  [("hres", t)], []))

        deferred.append(final_norm)

    stores = []
    for pb in range(n_pro):
        nb_ = (pb + 1, False, xp) if pb + 1 < n_pro else ((0, True, xm) if n_main > 0 else None)
        block(pb, xp, False, False, None, False, nb_)
    for q in range(4):
        qs = slice(q * 512, (q + 1) * 512)
        vsmul(hst[:, qs], hst[:, qs], c1(C_FLAG), [("hst", q), "cst"], [("hst", q)])
        act(hbf[:, qs], hst[:, qs], AF.Copy, [("hst", q)], [("hbf", q)])
    vcopy(nTh[:, :, 0:3], nT[:, :, TB - 3:TB], NTALL, ["nTh"])
    for bi in range(n_main):
        block(bi, xm, True, bi == 0 and first_halo, bi * NT, bi > 0, (bi + 1, True, xm) if bi + 1 < n_main else None)
    while deferred:
        deferred.pop(0)()
    S.add("sp", None, extra=stores)

    import contextlib
    with contextlib.ExitStack() as es:
        sems = {e: es.enter_context(nc.semaphore(f"sem_{e}")) for e in Sched.ENG}
        dsems = {sl: es.enter_context(nc.semaphore(f"dsem_{sl}")) for sl in S.slots}
        blk = es.enter_context(nc.Block())
        S.emit_all(nc, blk, sems, dsems)
    return nc


_CACHE = {}


def _consts(inp, flag):
    c = np.zeros((P, NCST), np.float32)
    k = np.arange(P)
    c[:, C_MGT:C_MGT + P] = (k[:, None] > k[None, :]).astype(np.float32)
    c[:, C_MLE:C_MLE + P] = (k[:, None] <= k[None, :]).astype(np.float32)
    c[:, C_ONE:C_ONE + P] = 1.0
    c[:, C_GMIX:C_GMIX + KD] = inp["norm_mix_g"][0].reshape(KD, P).T
    c[:, C_GFFN:C_GFFN + KD] = inp["norm_ffn_g"][0].reshape(KD, P).T
    c[:, C_GSSM:C_GSSM + KD] = inp["ssm_norm_g"][0].reshape(KD, P).T
    c[:, C_CW:C_CW + 128] = inp["ssm_conv_w"][0].reshape(4, 32, P).transpose(2, 1, 0).reshape(P, 128)
    c[:, C_CB:C_CB + 32] = inp["ssm_conv_b"][0].reshape(32, P).T
    c[:, C_SCW:C_SCW + 48] = inp["sc_conv_w"][0].reshape(3, 16, P).transpose(2, 1, 0).reshape(P, 48)
    c[:, C_DTB:C_DTB + NH] = inp["ssm_dt_bias"][0][None, :]
    c[:, C_ALOG:C_ALOG + NH] = inp["ssm_A_log"][0][None, :]
    c[:, C_DSK:C_DSK + NH] = inp["ssm_D"][0][None, :]
    c[:, C_FLAG] = flag
    c[:, C_ID:C_ID + P] = np.eye(P, dtype=np.float32)
    c[:, C_DCH:C_DCH + KD] = np.repeat(inp["ssm_D"][0], HD).reshape(KD, P).T
    return c


def kernel(**inputs):
    inp = {k: np.asarray(v, dtype=np.float32) for k, v in inputs.items()}
    x = inp["x"]
    if "nc" not in _CACHE:
        _CACHE["nc"] = build_program()
    nc = _CACHE["nc"]
    shared = {
        "w_in": np.ascontiguousarray(inp["w_in"][0]),
        "w_out": np.ascontiguousarray(inp["w_out"][0]),
        "w_gate": np.ascontiguousarray(inp["w_gate"][0]),
        "w_up": np.ascontiguousarray(inp["w_up"][0]),
        "w_down": np.ascontiguousarray(inp["w_down"][0]),
        "fg": np.ascontiguousarray(inp["norm_final_g"].reshape(1, D)),
    }
    in_maps = []
    for c in range(8):
        b, half = c // 2, c % 2
        m = dict(shared)
        m["xm"] = np.ascontiguousarray(x[b, half * TOK:(half + 1) * TOK])
        m["xp"] = np.ascontiguousarray(x[b, (1 - half) * TOK:(2 - half) * TOK])
        m["cst"] = _consts(inp, float(half))
        in_maps.append(m)
    res = run_bass_kernel_spmd(nc, in_maps, core_ids=list(range(8)))
    outp = np.empty((4, 2 * TOK, D), np.float32)
    for c in range(8):
        b, half = c // 2, c % 2
        outp[b, half * TOK:(half + 1) * TOK] = res.results[c]["out"]
    return outp
```

```python
import numpy as np
import concourse.bass as bass
import concourse.mybir as mybir
from concourse.bass_utils import run_bass_kernel_spmd

F32 = mybir.dt.float32
BF16 = mybir.dt.bfloat16
AF = mybir.ActivationFunctionType
ALU = mybir.AluOpType
AX = mybir.AxisListType

P = 128
D = 2048
KD = 16
DFF = 5632
KF = 44
NH = 32
HD = 64
NG = 8
DIN = 12320
OFF_Z, OFF_X, OFF_B, OFF_C, OFF_DT, OFF_CB, OFF_CC, OFF_CX = 0, 2048, 4096, 5120, 6144, 6176, 8224, 10272
TOK = 2048
TB = 512
NT = 4
NBLK = TOK // TB
EPS = 1e-5
NW = 3

C_MGT, C_MLE, C_ONE = 0, 128, 256
C_GMIX, C_GFFN, C_GSSM = 384, 400, 416
C_CW, C_CB, C_SCW = 432, 560, 592
C_DTB, C_ALOG, C_DSK, C_FLAG, C_ID = 640, 672, 704, 736, 737
C_DCH = 865
NCST = 881

SB_BASE = 16512
SB_END = 229344


class Op:
    __slots__ = ("eng", "emit", "deps", "tok", "signal", "num", "is_dma")


class Sched:
    ENG = ["pe", "act", "dve", "pool", "sp"]

    def __init__(self):
        self.ops = {e: [] for e in self.ENG}
        self.writers = {}
        self.readers = {}
        self.slots = {}
        self.all_dma = []

    def add(self, eng, emit, reads=(), writes=(), slot=None, extra=()):
        op = Op()
        op.eng, op.emit, op.signal, op.num = eng, emit, False, 0
        op.is_dma = slot is not None
        op.tok = None
        deps = set(extra)
        reads, writes = list(reads), list(writes)
        for r in list(reads):
            if isinstance(r, tuple) and r[0] == "ps":
                reads.remove(r)
                if r not in writes:
                    writes.append(r)
        for r in reads:
            for w in self.writers.get(r, {}).values():
                deps.add(w)
        for r in writes:
            isps = isinstance(r, tuple) and r[0] == "ps"
            for w in self.writers.get(r, {}).values():
                if w.is_dma or op.is_dma or w.eng != eng or (eng != "pe" and not isps):
                    deps.add(w)
            for rd in self.readers.get(r, {}).values():
                if rd.is_dma or op.is_dma or rd.eng != eng or (eng != "pe" and not isps):
                    deps.add(rd)
        for r in writes:
            self.readers[r] = {}
            self.writers.setdefault(r, {})[("dma", slot) if op.is_dma else eng] = op
        for r in reads:
            self.readers.setdefault(r, {})[("dma", slot) if op.is_dma else eng] = op
        for d in deps:
            if not d.is_dma:
                d.signal = True
        op.deps = deps
        if op.is_dma:
            n = self.slots.get(slot, 0) + 1
            self.slots[slot] = n
            op.tok = (slot, 16 * n)
            self.all_dma.append(op)
        self.ops[eng].append(op)
        return op

    def barrier(self, engs=("pe", "act", "dve", "sp")):
        lasts = []
        for e in engs:
            real = [o for o in self.ops[e] if o.emit is not None and not o.is_dma]
            if real:
                lasts.append(real[-1])
        dmas = list(self.all_dma)
        self.all_dma = []
        for e in engs:
            deps = [o for o in lasts if o.eng != e] + dmas
            self.add(e, None, extra=deps)

    def emit_all(self, nc, block, sems, dsems):
        for e in self.ENG:
            n = 0
            for op in self.ops[e]:
                if op.signal:
                    n += 1
                    op.num = n
        reg = {"pe": block.tensor, "act": block.scalar, "dve": block.vector,
               "pool": block.gpsimd, "sp": block.sync}
        for e in self.ENG:
            ops = self.ops[e]
            if not ops:
                continue

            def body(eng, e=e, ops=ops):
                waited = {}
                for op in ops:
                    for d in op.deps:
                        if d.is_dma:
                            key, val, sem = ("d", d.tok[0]), d.tok[1], dsems[d.tok[0]]
                        else:
                            key, val, sem = ("e", d.eng), d.num, sems[d.eng]
                        if waited.get(key, 0) < val:
                            waited[key] = val
                            eng.wait_ge(sem, val)
                    if op.emit is None:
                        continue
                    ins = op.emit(eng)
                    if op.is_dma:
                        ins.then_inc(dsems[op.tok[0]], 16)
                    elif op.signal:
                        ins.then_inc(sems[e], 1)

            reg[e](body)


def build_program(n_pro=NBLK, n_main=NBLK, stop=99, first_halo=True):
    nc = bass.Bass("TRN2", target_bir_lowering=False)
    xm = nc.dram_tensor("xm", [TOK, D], F32, kind="ExternalInput").ap()
    xp = nc.dram_tensor("xp", [TOK, D], F32, kind="ExternalInput").ap()
    w_in = nc.dram_tensor("w_in", [D, DIN], F32, kind="ExternalInput").ap()
    w_out = nc.dram_tensor("w_out", [2 * D, D], F32, kind="ExternalInput").ap()
    w_gate = nc.dram_tensor("w_gate", [D, DFF], F32, kind="ExternalInput").ap()
    w_up = nc.dram_tensor("w_up", [D, DFF], F32, kind="ExternalInput").ap()
    w_down = nc.dram_tensor("w_down", [DFF, D], F32, kind="ExternalInput").ap()
    cst_d = nc.dram_tensor("cst", [P, NCST], F32, kind="ExternalInput").ap()
    fg_d = nc.dram_tensor("fg", [1, D], F32, kind="ExternalInput").ap()
    out = nc.dram_tensor("out", [TOK, D], F32, kind="ExternalOutput").ap()

    S = Sched()
    off = [SB_BASE]

    def alloc(name, shape, dt, at=None):
        nb = int(np.prod(shape[1:])) * (4 if dt == F32 else 2)
        nb = (nb + 31) // 32 * 32
        if at is None:
            o = off[0]
            off[0] += nb
        else:
            o = at
        assert o + nb <= SB_END, (name, o, nb)
        return nc.alloc_sbuf_tensor_at(name, shape, dt, offset=o), o + nb

    def A(name, shape, dt):
        return alloc(name, shape, dt)[0]

    cst = A("cst", [P, NCST], F32)
    identb = A("identb", [P, P], BF16)
    a_bc = A("a_bc", [P, NH], F32)
    nT = A("nT", [P, KD, TB], BF16)
    nTh = A("nTh", [P, KD, 4], BF16)
    wbuf = [A(f"wbuf{i}", [P, KD, 512], BF16) for i in range(NW)]
    wdt = A("wdt", [P, KD, 32], BF16)
    hst = A("hst", [P, D], F32)
    hbf = A("hbf", [P, D], BF16)
    hist = A("hist", [P, 32, 3], F32)
    hist_sc = A("hist_sc", [P, 16, 2], F32)
    ss = A("ss", [P, 16], F32)
    rs = A("rs", [P, 16], F32)
    ssy = A("ssy", [P, 4], F32)
    dtx = [A(f"dtx{t}", [P, NH], F32) for t in range(NT)]
    dtl = [A(f"dtl{t}", [P, NH], F32) for t in range(NT)]
    dtv = [A(f"dtv{t}", [P, NH], F32) for t in range(NT)]
    dAv = [A(f"dAv{t}", [P, NH], F32) for t in range(NT)]
    dec3 = [A(f"dec3{t}", [P, 96], F32) for t in range(NT)]
    fac = [A(f"fac{t}", [P, NH], F32) for t in range(NT)]
    bdec = A("bdec", [P, NH], F32)
    diagD = A("diagD", [P, KD, P], BF16)
    AR = off[0]
    ymixT = A("ymixT", [P, 32, TB], BF16)
    scr8k = A("scr8k", [P, D], F32)
    xnb = [A(f"xn{i}", [P, D], BF16) for i in range(2)]
    sz = []
    SZ0_OFF = off[0]
    for t in range(NT):
        if t == 2:
            SZ2_OFF = off[0]
        sz.append(A(f"sz{t}", [P, D], BF16))
    xt2, _ = alloc("xt2", [P, D], F32, at=SZ0_OFF)
    fgbc, _ = alloc("fgbc", [P, D], F32, at=SZ2_OFF)
    AR_REST = off[0]
    BT = A("BT", [P, NG, TB], BF16)
    CT = A("CT", [P, NG, TB], BF16)
    Xb = A("Xb", [P, D], BF16)
    Xd = A("Xd", [P, D], BF16)
    Btm = A("Btm", [P, NG * P], BF16)
    Mb = [A(f"Mb{i}", [P, 512], BF16) for i in range(8)]
    Lb = [A(f"Lb{i}", [P, 512], BF16) for i in range(2)]
    cbm = [A(f"cbm{i}", [P, P], BF16) for i in range(2)]
    dat = [A(f"dat{i}", [P, 512], F32) for i in range(2)]
    NF2K = 5
    f2k = [A(f"f2k{i}", [P, 516], F32) for i in range(NF2K)]
    t1 = A("t1", [P, 512], F32)
    t3 = A("t3", [P, 512], BF16)
    MIX_END = off[0]
    o = AR_REST
    hres = []
    for t in range(NT):
        h_, o = alloc(f"hres{t}", [P, D], F32, at=o)
        hres.append(h_)
    actB, o = alloc("actB", [P, KF - 32, TB], BF16, at=o)
    actA, _ = alloc("actA", [P, 32, TB], BF16, at=AR)
    sg = []
    for i in range(4):
        s_, o = alloc(f"sg{i}", [P, 512], F32, at=o)
        sg.append(s_)
    assert max(o, MIX_END) <= SB_END, (o, MIX_END)
    print("SBUF end: mixer", MIX_END, "ffn", o, "limit", SB_END)

    def actT(k):
        return actA[:, k, :] if k < 32 else actB[:, k - 32, :]

    ps = [nc.alloc_psum_tensor(f"ps{i}", [P, 512], F32) for i in range(8)]
    psb = [p_[:].bitcast(BF16) for p_ in ps]
    rot = {"A": 0, "M": 0, "f2k": 0, "w": 0, "Mb": 0, "Lb": 0, "cbm": 0, "dat": 0, "os": 0, "xn": 0}

    pools = {"A": [0, 1, 2, 3], "M": [4, 5, 6, 7]}

    def set_pools(na):
        pools["A"] = list(range(na))
        pools["M"] = list(range(na, 8))

    def bankA():
        rot["A"] = (rot["A"] + 1) % len(pools["A"])
        return pools["A"][rot["A"]]

    def bankM():
        rot["M"] = (rot["M"] + 1) % len(pools["M"])
        return pools["M"][rot["M"]]

    def nxt(name, n):
        rot[name] = (rot[name] + 1) % n
        return rot[name]

    def c1(col, n=1):
        return cst[:, col:col + n]

    def mm(out_, lhsT, rhs, start, stop, reads, writes):
        S.add("pe", lambda e: e.matmul(out_, lhsT, rhs, start=start, stop=stop), reads, writes)

    def tr(out_, in_, reads, writes):
        S.add("pe", lambda e: e.transpose(out_, in_, identb[:]), list(reads) + ["identb"], writes)

    def act(out_, in_, func, reads, writes, **kw):
        S.add("act", lambda e: e.activation(out=out_, in_=in_, func=func, **kw), reads, writes)

    def vtt(out_, in0, in1, op, reads, writes):
        S.add("dve", lambda e: e.tensor_tensor(out=out_, in0=in0, in1=in1, op=op), reads, writes)

    def vstt(out_, in0, scalar, in1, op0, op1, reads, writes):
        S.add("dve", lambda e: e.scalar_tensor_tensor(out=out_, in0=in0, scalar=scalar, in1=in1,
                                                      op0=op0, op1=op1), reads, writes)

    def vsmul(out_, in0, scalar, reads, writes):
        S.add("dve", lambda e: e.tensor_scalar_mul(out=out_, in0=in0, scalar1=scalar), reads, writes)

    def vcopy(out_, in_, reads, writes):
        S.add("dve", lambda e: e.tensor_copy(out=out_, in_=in_), reads, writes)

    def dma(q, out_, in_, slot, reads, writes):
        return S.add(q, lambda e: e.dma_start(out=out_, in_=in_), reads, writes, slot=slot)

    def wload(src, nk, ncols):
        s = nxt("w", NW)
        dma("pool", wbuf[s][:, 0:nk, 0:ncols], src, f"w{s}", [], [("w", s)])
        return s

    def wsrc(w, r0, nk, c0, ncols):
        return w[r0:r0 + nk * P, c0:c0 + ncols].rearrange("(k p) c -> p k c", p=P)

    dma("sp", cst[:], cst_d, "cst", [], ["cst"])
    vcopy(identb[:], cst[:, C_ID:C_ID + P], ["cst"], ["identb"])
    act(a_bc[:], c1(C_ALOG, NH), AF.Exp, ["cst"], ["a_bc"])
    vsmul(a_bc[:], a_bc[:], -1.0, ["a_bc"], ["a_bc"])
    for k_ in range(KD):
        vsmul(diagD[:, k_, :], cst[:, C_ID:C_ID + P], c1(C_DCH + k_), ["cst"], ["diagD"])
    S.add("dve", lambda e: e.memset(hst[:], 0.0), [], [("hst", q_) for q_ in range(4)])
    S.add("dve", lambda e: e.memset(hbf[:], 0.0), [], [("hbf", q_) for q_ in range(4)])
    S.add("dve", lambda e: e.memset(hist[:], 0.0), [], [("hist", c_) for c_ in range(32)])
    S.add("dve", lambda e: e.memset(hist_sc[:], 0.0), [], [("hsc", c_) for c_ in range(16)])

    def rstd_ops(c):
        act(rs[:, c:c + 1], ss[:, c:c + 1], AF.Ln, [("ss", c)], [("rs", c)], scale=1.0 / D, bias=EPS)
        act(rs[:, c:c + 1], rs[:, c:c + 1], AF.Exp, [("rs", c)], [("rs", c)], scale=-0.5)

    NTALL = [("nT", t) for t in range(NT)]

    def norm_stage1(src, sres, c):
        xi = nxt("xn", 2)
        xbuf, xres = xnb[xi], [("xn", xi)]
        act(xbuf[:], src, AF.Square, sres, xres + [("ss", c)], accum_out=ss[:, c:c + 1])
        rstd_ops(c)
        vsmul(xbuf[:], src, rs[:, c:c + 1], sres + [("rs", c)], xres)
        return (xbuf, xres)

    def norm_to_nT(src_fn, src_res_fn, gcol, col0, pre=None):
        bufs = dict(pre or {})

        def stage1(t):
            if t in bufs:
                return
            bufs[t] = norm_stage1(src_fn(t), src_res_fn(t), col0 + t)

        def stage2(t):
            xbuf, xres = bufs[t]
            for half in range(2):
                b = bankM()
                for kk in range(8):
                    k = half * 8 + kk
                    tr(psb[b][:, kk * P:(kk + 1) * P], xbuf[:, k * P:(k + 1) * P], xres, [("ps", b)])
                k0 = half * 8
                vtt(nT[:, k0:k0 + 8, t * P:(t + 1) * P], psb[b][:, 0:1024].rearrange("p (k c) -> p k c", k=8),
                    cst[:, gcol + k0:gcol + k0 + 8].unsqueeze(2).to_broadcast([P, 8, P]), ALU.mult,
                    [("ps", b), "cst"], [("nT", t)])

        stage1(0)
        for t in range(NT):
            if t + 1 < NT:
                stage1(t + 1)
            stage2(t)

    def formW(s, e, nk=KD):
        b = bankA()
        for k in range(nk):
            mm(ps[b][:], wbuf[s][:, k, e * P:(e + 1) * P], nT[:, k, :], k == 0, k == nk - 1,
               [("w", s)] + NTALL, [("ps", b)])
        return b

    def halo_mm(s, e, ncol):
        b = bankM()
        for k in range(KD):
            mm(ps[b][:, 0:ncol], wbuf[s][:, k, e * P:(e + 1) * P], nTh[:, k, 3 - ncol:3], k == 0, k == KD - 1,
               [("w", s), "nTh"], [("ps", b)])
        return b

    def conv_front(ch, b, halo_b):
        ui = nxt("f2k", NF2K)
        ai = nxt("f2k", NF2K)
        ue, a = f2k[ui], f2k[ai]
        ur, ar = ("f2k", ui), ("f2k", ai)
        if halo_b is not None:
            act(hist[:, ch, :], ps[halo_b][:, 0:3], AF.Copy, [("ps", halo_b), "cst"], [("hist", ch)], scale=c1(C_FLAG))
        act(ue[:, 0:3], hist[:, ch, :], AF.Copy, [("hist", ch)], [ur])
        act(ue[:, 3:515], ps[b][:], AF.Copy, [("ps", b)], [ur])
        act(a[:, 0:512], ps[b][:], AF.Identity, [("ps", b), "cst"], [ar],
            scale=c1(C_CW + ch * 4 + 3), bias=c1(C_CB + ch))
        act(hist[:, ch, :], ue[:, 512:515], AF.Copy, [ur], [("hist", ch)])
        for tap in (2, 1, 0):
            vstt(a[:, 0:512], ue[:, tap:tap + 512], c1(C_CW + ch * 4 + tap), a[:, 0:512], ALU.mult, ALU.add,
                 [ur, ar, "cst"], [ar])
        return ai

    def conv_back(dst, dres, ai):
        act(dst, f2k[ai][:, 0:512], AF.Silu, [("f2k", ai)], dres)

    yn_idx = {}
    FGK = [("sz", t_, j_) for t_ in (2, 3) for j_ in range(4)]
    XT2K = [("sz", t_, j_) for t_ in (0, 1) for j_ in range(4)]

    prefetched = set()
    hoisted = {}
    deferred = []

    def block(bi, xsrc, main, first_main, out_rows, barrier_after_z, nxt_blk):
        def xsrc_fn(t):
            buf, keys = (scr8k, ["scr8k"]) if t % 2 == 0 else (xt2, XT2K)
            if (bi, main, t) not in prefetched:
                dma("sp", buf[:], xsrc[(bi * NT + t) * P:(bi * NT + t + 1) * P, :], f"xt{t % 2}", [], keys)
            return buf[:]

        def prefetch_next():
            if nxt_blk is None:
                return
            nbi, nmain, nsrc = nxt_blk
            for t in range(2):
                buf, keys = (scr8k, ["scr8k"]) if t % 2 == 0 else (xt2, XT2K)
                dma("sp", buf[:], nsrc[(nbi * NT + t) * P:(nbi * NT + t + 1) * P, :], f"xt{t % 2}", [], keys)
                prefetched.add((nbi, nmain, t))

        def hoist_next_norm():
            if nxt_blk is None:
                return
            nbi, nmain, nsrc = nxt_blk
            d_ = {}
            for t in range(2):
                buf, keys = (scr8k, ["scr8k"]) if t % 2 == 0 else (xt2, XT2K)
                d_[t] = norm_stage1(buf[:], keys, t)
            hoisted[(nbi, nmain)] = d_

        norm_to_nT(xsrc_fn, lambda t: ["scr8k"] if t % 2 == 0 else XT2K, C_GMIX, 0, hoisted.pop((bi, main), None))
        while deferred:
            deferred.pop(0)()

        dma("pool", wdt[:], wsrc(w_in, 0, KD, OFF_DT, 32), "wdt", [], ["wdt"])
        set_pools(3)

        def dt_part1():
            dtb_ = []
            for t in range(NT):
                b = bankM()
                for k in range(KD):
                    mm(ps[b][:, 0:NH], nT[:, k, t * P:(t + 1) * P], wdt[:, k, :], k == 0, k == KD - 1,
                       ["wdt", ("nT", t)], [("ps", b)])
                dtb_.append(b)
            for t in range(NT):
                b = dtb_[t]
                vtt(dtx[t][:], ps[b][:, 0:NH], c1(C_DTB, NH), ALU.add, [("ps", b), "cst"], [("dtx", t)])
                vstt(dtl[t][:], dtx[t][:], -1.0, dtx[t][:], ALU.mult, ALU.max, [("dtx", t)], [("dtl", t)])
                act(dtl[t][:], dtl[t][:], AF.Exp, [("dtl", t)], [("dtl", t)], scale=-1.0)
                act(dtl[t][:], dtl[t][:], AF.Ln, [("dtl", t)], [("dtl", t)], bias=1.0)
                vstt(dtv[t][:], dtx[t][:], 0.0, dtl[t][:], ALU.max, ALU.add, [("dtx", t), ("dtl", t)], [("dtv", t)])
                vtt(dAv[t][:], dtv[t][:], a_bc[:], ALU.mult, [("dtv", t), "a_bc"], [("dAv", t)])

        if not main:
            dt_part1()

        def dt_part2():
            for t in range(NT):
                b2 = bankM()
                for i, mcol in enumerate((C_MGT, C_MLE, C_ONE)):
                    mm(ps[b2][:, i * NH:(i + 1) * NH], cst[:, mcol:mcol + P], dAv[t][:], True, True,
                       ["cst", ("dAv", t)], [("ps", b2)])
                act(dec3[t][:], ps[b2][:, 0:96], AF.Exp, [("ps", b2)], [("dec3", t)])

        if main:
            for j in range(4):
                s = wload(wsrc(w_in, 0, KD, OFF_Z + j * 512, 512), KD, 512)
                for t in range(NT):
                    b = bankA()
                    for k in range(KD):
                        mm(ps[b][:], nT[:, k, t * P:(t + 1) * P], wbuf[s][:, k, :], k == 0, k == KD - 1,
                           [("w", s), ("nT", t)], [("ps", b)])
                    act(sz[t][:, j * 512:(j + 1) * 512], ps[b][:], AF.Silu, [("ps", b)], [("sz", t, j)])
                if j == 0:
                    dt_part1()
            dt_part2()
        if barrier_after_z:
            S.barrier()
        if main and stop <= 1:
            return
        secs = [(OFF_X, 4, 0), (OFF_B, 2, 16)] + ([(OFF_C, 2, 24)] if main else [])
        pending = None
        for c0, nb, ch0 in secs:
            for jb in range(nb):
                s = wload(wsrc(w_in, 0, KD, c0 + jb * 512, 512), KD, 512)
                for e in range(4):
                    ch = ch0 + jb * 4 + e
                    b = formW(s, e)
                    hb = halo_mm(s, e, 3) if first_main else None
                    if ch < 16:
                        dst, dres = ymixT[:, ch, :], [("ymT", ch, tt) for tt in range(NT)]
                    elif ch < 24:
                        dst, dres = BT[:, ch - 16, :], [("BT", ch - 16)]
                    else:
                        dst, dres = CT[:, ch - 24, :], [("CT", ch - 24)]
                    ai = conv_front(ch, b, hb)
                    if pending is not None:
                        conv_back(*pending)
                    pending = (dst, dres, ai)
                if (not main) and c0 == OFF_X and jb == 0:
                    dt_part2()
        conv_back(*pending)
        if main and stop <= 2:
            return

        def ssd1_pre(t):
            tc = slice(t * P, (t + 1) * P)
            for half in range(2):
                b = bankM()
                for kk in range(8):
                    k = half * 8 + kk
                    tr(psb[b][:, kk * P:(kk + 1) * P], ymixT[:, k, tc], [("ymT", k, t)], [("ps", b)])
                hs = slice(half * 1024, (half + 1) * 1024)
                dtb = dtv[t][:, half * 16:(half + 1) * 16].unsqueeze(2).to_broadcast([P, 16, HD])
                vtt(Xb[:, hs].rearrange("p (h d) -> p h d", h=16),
                    psb[b][:, 0:1024].rearrange("p (h d) -> p h d", h=16), dtb, ALU.mult,
                    [("ps", b), ("dtv", t)], [("Xb", half)])
                vtt(Xd[:, hs].rearrange("p (h d) -> p h d", h=16),
                    Xb[:, hs].rearrange("p (h d) -> p h d", h=16),
                    dec3[t][:, half * 16:(half + 1) * 16].unsqueeze(2).to_broadcast([P, 16, HD]), ALU.mult,
                    [("Xb", half), ("dec3", t)], [("Xd", half)])
                yield
            b = bankM()
            for g in range(NG):
                tr(psb[b][:, g * P:(g + 1) * P], BT[:, g, tc], [("BT", g)], [("ps", b)])
            act(Btm[:], psb[b][:, 0:1024], AF.Copy, [("ps", b)], ["Btm"])
            yield

        def state_update(t):
            for q in range(4):
                qs = slice(q * 512, (q + 1) * 512)
                b = bankM()
                for g2 in range(2):
                    g = q * 2 + g2
                    mm(ps[b][:, g2 * 256:(g2 + 1) * 256], Btm[:, g * P:(g + 1) * P], Xd[:, g * 256:(g + 1) * 256],
                       True, True, ["Btm", ("Xd", q // 2)], [("ps", b)])
                vtt(hst[:, qs].rearrange("p (h d) -> p h d", h=8), hst[:, qs].rearrange("p (h d) -> p h d", h=8),
                    dec3[t][:, 64 + q * 8:64 + q * 8 + 8].unsqueeze(2).to_broadcast([P, 8, HD]), ALU.mult,
                    [("hst", q), ("dec3", t)], [("hst", q)])
                vtt(hst[:, qs], hst[:, qs], ps[b][:], ALU.add, [("hst", q), ("ps", b)], [("hst", q)])
                act(hbf[:, qs], hst[:, qs], AF.Copy, [("hst", q)], [("hbf", q)])
                yield

        def stageA(t):
            tc = slice(t * P, (t + 1) * P)

            def mk_dat(g, tt=None):
                tt = t if tt is None else tt
                di = g % 2
                for hh in range(4):
                    act(dat[di][:, hh * P:(hh + 1) * P], cst[:, C_MLE:C_MLE + P], AF.Copy,
                        ["cst", ("dAv", tt)], [("dat", di)], scale=dAv[tt][:, 4 * g + hh:4 * g + hh + 1])

            def group(g):
                di = li = ci = g % 2
                b = bankM()
                mm(ps[b][:], cst[:, C_MGT:C_MGT + P], dat[di][:], True, True, ["cst", ("dat", di)], [("ps", b)])
                act(Lb[li][:], ps[b][:], AF.Exp, [("ps", b)], [("Lb", li)])
                b2 = bankM()
                mm(ps[b2][:, 0:P], BT[:, g, tc], CT[:, g, tc], True, True, [("BT", g), ("CT", g)], [("ps", b2)])
                if g + 2 < NG:
                    mk_dat(g + 2)
                vtt(cbm[ci][:], ps[b2][:, 0:P], cst[:, C_MLE:C_MLE + P], ALU.mult, [("ps", b2), "cst"], [("cbm", ci)])
                vtt(Mb[g][:].rearrange("p (h l) -> p h l", h=4),
                    Lb[li][:].rearrange("p (h l) -> p h l", h=4),
                    cbm[ci][:].unsqueeze(1).to_broadcast([P, 4, P]), ALU.mult,
                    [("Lb", li), ("cbm", ci)], [("Mb", g)])

            if t == 0:
                mk_dat(0, t)
                mk_dat(1, t)
            pre = ssd1_pre(t)
            for g in range(NG):
                group(g)
                yield
                if g % 2 == 1:
                    next(pre, None)
                    yield
            for _ in pre:
                yield
            if t + 1 < NT:
                mk_dat(0, t + 1)
                mk_dat(1, t + 1)

        def stageB(t):
            tc = slice(t * P, (t + 1) * P)
            for q in range(4):
                half = q // 2
                qs = slice(q * 512, (q + 1) * 512)
                bd = bankM()
                for j in range(4):
                    k = q * 4 + j
                    S.add("pe", lambda e, bd=bd, j=j, k=k: e.matmul(
                        ps[bd][:, j * P:(j + 1) * P], ymixT[:, k, tc], diagD[:, k, :], start=(j == 0), stop=False,
                        skip_group_check=True), [("ymT", k, t), "diagD"], [("ps", bd)])
                for hh in range(8):
                    h = q * 8 + hh
                    g = h // 4
                    S.add("pe", lambda e, bd=bd, hh=hh, h=h, g=g: e.matmul(
                        ps[bd][:, hh * HD:(hh + 1) * HD], Mb[g][:, (h % 4) * P:(h % 4 + 1) * P],
                        Xb[:, h * HD:(h + 1) * HD], start=False, stop=(hh == 7), skip_group_check=True),
                        [("Mb", g), ("Xb", half)], [("ps", bd)])
                bo = bankM()
                for g2 in range(2):
                    g = q * 2 + g2
                    mm(ps[bo][:, g2 * 256:(g2 + 1) * 256], CT[:, g, tc], hbf[:, g * 256:(g + 1) * 256], True, True,
                       [("CT", g), ("hbf", q)], [("ps", bo)])
                vtt(t1[:].rearrange("p (h d) -> p h d", h=8), ps[bo][:].rearrange("p (h d) -> p h d", h=8),
                    dec3[t][:, 32 + q * 8:32 + q * 8 + 8].unsqueeze(2).to_broadcast([P, 8, HD]), ALU.mult,
                    [("ps", bo), ("dec3", t)], ["t1"])
                vtt(t1[:], ps[bd][:], t1[:], ALU.add, [("ps", bd), "t1"], ["t1"])
                vtt(scr8k[:, qs], t1[:], sz[t][:, qs], ALU.mult, ["t1", ("sz", t, q)], ["scr8k"])
                act(t3[:], scr8k[:, qs], AF.Square, ["scr8k"], ["t3", ("ssy", q)], accum_out=ssy[:, q:q + 1])
                yield
            c = 8 + t
            S.add("dve", lambda e, c=c: e.reduce_sum(out=ss[:, c:c + 1], in_=ssy[:, 0:4], axis=AX.X),
                  [("ssy", q) for q in range(4)], [("ss", c)])
            rstd_ops(c)
            yi = nxt("xn", 2)
            yn_idx[t] = yi
            act(xnb[yi][:], scr8k[:], AF.Copy, ["scr8k", ("rs", c)], [("xn", yi)], scale=rs[:, c:c + 1])
            yield
            yield from state_update(t)

        def y_transposes(t):
            tc = slice(t * P, (t + 1) * P)
            yi = yn_idx[t]
            for half in range(2):
                b = bankM()
                for kk in range(8):
                    k = half * 8 + kk
                    tr(psb[b][:, kk * P:(kk + 1) * P], xnb[yi][:, k * P:(k + 1) * P], [("xn", yi)], [("ps", b)])
                k0 = half * 8
                vtt(ymixT[:, k0:k0 + 8, tc], psb[b][:, 0:1024].rearrange("p (k c) -> p k c", k=8),
                    cst[:, C_GSSM + k0:C_GSSM + k0 + 8].unsqueeze(2).to_broadcast([P, 8, P]), ALU.mult,
                    [("ps", b), "cst"], [("ymT", k0 + kk, t) for kk in range(8)])
                yield

        def sc_group(j):
            s = wload(wsrc(w_in, 0, KD, OFF_CC + j * 512, 512), KD, 512)
            ccs = []
            for e in range(4):
                b = formW(s, e)
                hb = halo_mm(s, e, 2) if first_main else None
                fi = nxt("f2k", NF2K)
                act(f2k[fi][:, 0:512], ps[b][:], AF.Copy, [("ps", b)], [("f2k", fi)])
                ccs.append(fi)
                if hb is not None:
                    act(hist_sc[:, j * 4 + e, :], ps[hb][:, 0:2], AF.Copy, [("ps", hb), "cst"], [("hsc", j * 4 + e)],
                        scale=c1(C_FLAG))
                yield
            s = wload(wsrc(w_in, 0, KD, OFF_CX + j * 512, 512), KD, 512)
            for e in range(4):
                ch = j * 4 + e
                b = formW(s, e)
                if first_main:
                    hb = halo_mm(s, e, 2)
                    vtt(hist_sc[:, ch, :], hist_sc[:, ch, :], ps[hb][:, 0:2], ALU.mult, [("hsc", ch), ("ps", hb)], [("hsc", ch)])
                gi = nxt("f2k", NF2K)
                while gi in ccs:
                    gi = nxt("f2k", NF2K)
                ge = f2k[gi]
                a = f2k[ccs[e]]
                vcopy(ge[:, 0:2], hist_sc[:, ch, :], [("hsc", ch)], [("f2k", gi)])
                vtt(ge[:, 2:514], ps[b][:], a[:, 0:512], ALU.mult, [("ps", b), ("f2k", ccs[e])], [("f2k", gi)])
                act(a[:, 0:512], ge[:, 2:514], AF.Copy, [("f2k", gi), "cst"], [("f2k", ccs[e])], scale=c1(C_SCW + ch * 3 + 2))
                vcopy(hist_sc[:, ch, :], ge[:, 512:514], [("f2k", gi)], [("hsc", ch)])
                for tap in (1, 0):
                    vstt(a[:, 0:512], ge[:, tap:tap + 512], c1(C_SCW + ch * 3 + tap), a[:, 0:512], ALU.mult, ALU.add,
                         [("f2k", gi), ("f2k", ccs[e]), "cst"], [("f2k", ccs[e])])
                yield
            s = wload(wsrc(w_in, 0, KD, OFF_CB + j * 512, 512), KD, 512)
            for e in range(4):
                ch = j * 4 + e
                b = formW(s, e)
                vtt(ymixT[:, 16 + ch, :], ps[b][:], f2k[ccs[e]][:, 0:512], ALU.mult, [("ps", b), ("f2k", ccs[e])],
                    [("ymT", 16 + ch, tt) for tt in range(NT)])
                yield

        def run(gen):
            for _ in gen:
                pass

        if not main:
            set_pools(4)
            accb = [bankA() for _ in range(4)]
            for t in range(NT):
                vtt(fac[t][:], dtv[t][:], dec3[t][:, 0:NH], ALU.mult, [("dtv", t), ("dec3", t)], [("fac", t)])
                for t2 in range(t + 1, NT):
                    vtt(fac[t][:], fac[t][:], dec3[t2][:, 64:96], ALU.mult, [("fac", t), ("dec3", t2)], [("fac", t)])
            vtt(bdec[:], dec3[0][:, 64:96], dec3[1][:, 64:96], ALU.mult, [("dec3", 0), ("dec3", 1)], ["bdec"])
            for t2 in (2, 3):
                vtt(bdec[:], bdec[:], dec3[t2][:, 64:96], ALU.mult, ["bdec", ("dec3", t2)], ["bdec"])
            for t in range(NT):
                tc = slice(t * P, (t + 1) * P)
                for half in range(2):
                    b = bankM()
                    for kk in range(8):
                        k = half * 8 + kk
                        tr(psb[b][:, kk * P:(kk + 1) * P], ymixT[:, k, tc], [("ymT", k, t)], [("ps", b)])
                    hs = slice(half * 1024, (half + 1) * 1024)
                    vtt(Xd[:, hs].rearrange("p (h d) -> p h d", h=16),
                        psb[b][:, 0:1024].rearrange("p (h d) -> p h d", h=16),
                        fac[t][:, half * 16:(half + 1) * 16].unsqueeze(2).to_broadcast([P, 16, HD]), ALU.mult,
                        [("ps", b), ("fac", t)], [("Xd", half)])
                b = bankM()
                for g in range(NG):
                    tr(psb[b][:, g * P:(g + 1) * P], BT[:, g, tc], [("BT", g)], [("ps", b)])
                act(Btm[:], psb[b][:, 0:1024], AF.Copy, [("ps", b)], ["Btm"])
                for q in range(4):
                    for g2 in range(2):
                        g = q * 2 + g2
                        S.add("pe", lambda e, q=q, g2=g2, g=g, t=t: e.matmul(
                            ps[accb[q]][:, g2 * 256:(g2 + 1) * 256], Btm[:, g * P:(g + 1) * P],
                            Xd[:, g * 256:(g + 1) * 256], start=(t == 0 and g2 == 0),
                            stop=(t == NT - 1 and g2 == 1), skip_group_check=True),
                            ["Btm", ("Xd", q // 2)], [("ps", accb[q])])
                if t == 0:
                    prefetch_next()
                if t == 2:
                    hoist_next_norm()
            for q in range(4):
                qs = slice(q * 512, (q + 1) * 512)
                vtt(hst[:, qs].rearrange("p (h d) -> p h d", h=8), hst[:, qs].rearrange("p (h d) -> p h d", h=8),
                    bdec[:, q * 8:q * 8 + 8].unsqueeze(2).to_broadcast([P, 8, HD]), ALU.mult,
                    [("hst", q), "bdec"], [("hst", q)])
                vtt(hst[:, qs], hst[:, qs], ps[accb[q]][:], ALU.add, [("hst", q), ("ps", accb[q])], [("hst", q)])
            return

        def micro_steps():
            yield from stageA(0)
            for t in range(NT):
                yield from stageB(t)
                if t + 1 < NT:
                    yield from stageA(t + 1)
                if t > 0:
                    yield from y_transposes(t - 1)
            yield from y_transposes(NT - 2) if False else iter(())

        def sc_units():
            for j in range(4):
                yield from sc_group(j)

        micro = micro_steps()
        MICRO_PER_UNIT = 2
        for _ in sc_units():
            for _i in range(MICRO_PER_UNIT):
                next(micro, None)
        run(micro)
        if stop <= 5:
            return

        def outproj_part(j, kh, banks, first, last):
            s = wload(wsrc(w_out, kh * D, KD, j * 512, 512), KD, 512)
            for t in range(NT):
                for k in range(KD):
                    kg = kh * KD + k
                    mm(ps[banks[t]][:], ymixT[:, kg, t * P:(t + 1) * P], wbuf[s][:, k, :],
                       first and k == 0, last and k == KD - 1, [("w", s), ("ymT", kg, t)], [("ps", banks[t])])

        set_pools(4)
        banks = [bankA() for _ in range(NT)]
        outproj_part(0, 1, banks, True, False)
        run(y_transposes(NT - 1))
        S.barrier()
        for t in range(NT):
            dma("sp", hres[t][:], xsrc[(bi * NT + t) * P:(bi * NT + t + 1) * P, :], f"xr{t}", [], [("hres", t)])
        for j in range(4):
            if j > 0:
                banks = [bankA() for _ in range(NT)]
                outproj_part(j, 1, banks, True, False)
            outproj_part(j, 0, banks, False, True)
            for t in range(NT):
                vtt(hres[t][:, j * 512:(j + 1) * 512], hres[t][:, j * 512:(j + 1) * 512], ps[banks[t]][:], ALU.add,
                    [("hres", t), ("ps", banks[t])], [("hres", t)])
        S.barrier()
        if stop <= 6:
            return

        prefetch_next()
        dma("sp", fgbc[:], fg_d.partition_broadcast(P), "fg", [], FGK)
        norm_to_nT(lambda t: hres[t][:], lambda t: [("hres", t)], C_GFFN, 4)
        for j in range(KF // 4):
            s = wload(wsrc(w_gate, 0, KD, j * 512, 512), KD, 512)
            for e in range(4):
                b = formW(s, e)
                act(sg[e][:], ps[b][:], AF.Silu, [("ps", b)], [("sg", e)])
            s = wload(wsrc(w_up, 0, KD, j * 512, 512), KD, 512)
            for e in range(4):
                b = formW(s, e)
                vtt(actT(j * 4 + e), ps[b][:], sg[e][:], ALU.mult, [("ps", b), ("sg", e)], [("actT", j * 4 + e)])
        if stop <= 7:
            return
        hoist_next_norm()
        for j in range(4):
            banks = [bankA() for _ in range(NT)]
            for (k0, nk) in ((0, 16), (16, 16), (32, 12)):
                s = wload(wsrc(w_down, k0 * P, nk, j * 512, 512), nk, 512)
                for t in range(NT):
                    for k in range(nk):
                        kg = k0 + k
                        mm(ps[banks[t]][:], actT(kg)[:, t * P:(t + 1) * P], wbuf[s][:, k, :], kg == 0, kg == KF - 1,
                           [("w", s), ("actT", kg)], [("ps", banks[t])])
            for t in range(NT):
                vtt(hres[t][:, j * 512:(j + 1) * 512], hres[t][:, j * 512:(j + 1) * 512], ps[banks[t]][:], ALU.add,
                    [("hres", t), ("ps", banks[t])], [("hres", t)])
        def final_norm():
            for t in range(NT):
                c = 12 + t
                xi = nxt("xn", 2)
                act(xnb[xi][:], hres[t][:], AF.Square, [("hres", t)], [("xn", xi), ("ss", c)], accum_out=ss[:, c:c + 1])
                rstd_ops(c)
                vstt(hres[t][:], hres[t][:], rs[:, c:c + 1], fgbc[:], ALU.mult, ALU.mult,
                     [("hres", t), ("rs", c)] + FGK, [("hres", t)])
                stores.append(dma("sp", out[(out_rows + t) * P:(out_rows + t + 1) * P, :], hres[t][:], f"st{t}",
                                  [("hres", t)], []))

        deferred.append(final_norm)

    stores = []
    for pb in range(n_pro):
        nb_ = (pb + 1, False, xp) if pb + 1 < n_pro else ((0, True, xm) if n_main > 0 else None)
        block(pb, xp, False, False, None, False, nb_)
    for q in range(4):
        qs = slice(q * 512, (q + 1) * 512)
        vsmul(hst[:, qs], hst[:, qs], c1(C_FLAG), [("hst", q), "cst"], [("hst", q)])
        act(hbf[:, qs], hst[:, qs], AF.Copy, [("hst", q)], [("hbf", q)])
    vcopy(nTh[:, :, 0:3], nT[:, :, TB - 3:TB], NTALL, ["nTh"])
    for bi in range(n_main):
        block(bi, xm, True, bi == 0 and first_halo, bi * NT, bi > 0, (bi + 1, True, xm) if bi + 1 < n_main else None)
    while deferred:
        deferred.pop(0)()
    S.add("sp", None, extra=stores)

    import contextlib
    with contextlib.ExitStack() as es:
        sems = {e: es.enter_context(nc.semaphore(f"sem_{e}")) for e in Sched.ENG}
        dsems = {sl: es.enter_context(nc.semaphore(f"dsem_{sl}")) for sl in S.slots}
        blk = es.enter_context(nc.Block())
        S.emit_all(nc, blk, sems, dsems)
    return nc


_CACHE = {}


def _consts(inp, flag):
    c = np.zeros((P, NCST), np.float32)
    k = np.arange(P)
    c[:, C_MGT:C_MGT + P] = (k[:, None] > k[None, :]).astype(np.float32)
    c[:, C_MLE:C_MLE + P] = (k[:, None] <= k[None, :]).astype(np.float32)
    c[:, C_ONE:C_ONE + P] = 1.0
    c[:, C_GMIX:C_GMIX + KD] = inp["norm_mix_g"][0].reshape(KD, P).T
    c[:, C_GFFN:C_GFFN + KD] = inp["norm_ffn_g"][0].reshape(KD, P).T
    c[:, C_GSSM:C_GSSM + KD] = inp["ssm_norm_g"][0].reshape(KD, P).T
    c[:, C_CW:C_CW + 128] = inp["ssm_conv_w"][0].reshape(4, 32, P).transpose(2, 1, 0).reshape(P, 128)
    c[:, C_CB:C_CB + 32] = inp["ssm_conv_b"][0].reshape(32, P).T
    c[:, C_SCW:C_SCW + 48] = inp["sc_conv_w"][0].reshape(3, 16, P).transpose(2, 1, 0).reshape(P, 48)
    c[:, C_DTB:C_DTB + NH] = inp["ssm_dt_bias"][0][None, :]
    c[:, C_ALOG:C_ALOG + NH] = inp["ssm_A_log"][0][None, :]
    c[:, C_DSK:C_DSK + NH] = inp["ssm_D"][0][None, :]
    c[:, C_FLAG] = flag
    c[:, C_ID:C_ID + P] = np.eye(P, dtype=np.float32)
    c[:, C_DCH:C_DCH + KD] = np.repeat(inp["ssm_D"][0], HD).reshape(KD, P).T
    return c


def kernel(**inputs):
    inp = {k: np.asarray(v, dtype=np.float32) for k, v in inputs.items()}
    x = inp["x"]
    if "nc" not in _CACHE:
        _CACHE["nc"] = build_program()
    nc = _CACHE["nc"]
    shared = {
        "w_in": np.ascontiguousarray(inp["w_in"][0]),
        "w_out": np.ascontiguousarray(inp["w_out"][0]),
        "w_gate": np.ascontiguousarray(inp["w_gate"][0]),
        "w_up": np.ascontiguousarray(inp["w_up"][0]),
        "w_down": np.ascontiguousarray(inp["w_down"][0]),
        "fg": np.ascontiguousarray(inp["norm_final_g"].reshape(1, D)),
    }
    in_maps = []
    for c in range(8):
        b, half = c // 2, c % 2
        m = dict(shared)
        m["xm"] = np.ascontiguousarray(x[b, half * TOK:(half + 1) * TOK])
        m["xp"] = np.ascontiguousarray(x[b, (1 - half) * TOK:(2 - half) * TOK])
        m["cst"] = _consts(inp, float(half))
        in_maps.append(m)
    res = run_bass_kernel_spmd(nc, in_maps, core_ids=list(range(8)))
    outp = np.empty((4, 2 * TOK, D), np.float32)
    for c in range(8):
        b, half = c // 2, c % 2
        outp[b, half * TOK:(half + 1) * TOK] = res.results[c]["out"]
    return outp
```

```python
import numpy as np
import concourse.bass as bass
import concourse.mybir as mybir
from concourse.bass_utils import run_bass_kernel_spmd

F32 = mybir.dt.float32
BF16 = mybir.dt.bfloat16
AF = mybir.ActivationFunctionType
ALU = mybir.AluOpType
AX = mybir.AxisListType

P = 128
D = 2048
KD = 16
DFF = 5632
KF = 44
NH = 32
HD = 64
NG = 8
DIN = 12320
OFF_Z, OFF_X, OFF_B, OFF_C, OFF_DT, OFF_CB, OFF_CC, OFF_CX = 0, 2048, 4096, 5120, 6144, 6176, 8224, 10272
TOK = 2048
TB = 512
NT = 4
NBLK = TOK // TB
EPS = 1e-5
NW = 3

C_MGT, C_MLE, C_ONE = 0, 128, 256
C_GMIX, C_GFFN, C_GSSM = 384, 400, 416
C_CW, C_CB, C_SCW = 432, 560, 592
C_DTB, C_ALOG, C_DSK, C_FLAG, C_ID = 640, 672, 704, 736, 737
C_DCH = 865
NCST = 881

SB_BASE = 16512
SB_END = 229344


class Op:
    __slots__ = ("eng", "emit", "deps", "tok", "signal", "num", "is_dma")


class Sched:
    ENG = ["pe", "act", "dve", "pool", "sp"]

    def __init__(self):
        self.ops = {e: [] for e in self.ENG}
        self.writers = {}
        self.readers = {}
        self.slots = {}
        self.all_dma = []

    def add(self, eng, emit, reads=(), writes=(), slot=None, extra=()):
        op = Op()
        op.eng, op.emit, op.signal, op.num = eng, emit, False, 0
        op.is_dma = slot is not None
        op.tok = None
        deps = set(extra)
        reads, writes = list(reads), list(writes)
        for r in list(reads):
            if isinstance(r, tuple) and r[0] == "ps":
                reads.remove(r)
                if r not in writes:
                    writes.append(r)
        for r in reads:
            for w in self.writers.get(r, {}).values():
                deps.add(w)
        for r in writes:
            isps = isinstance(r, tuple) and r[0] == "ps"
            for w in self.writers.get(r, {}).values():
                if w.is_dma or op.is_dma or w.eng != eng or (eng != "pe" and not isps):
                    deps.add(w)
            for rd in self.readers.get(r, {}).values():
                if rd.is_dma or op.is_dma or rd.eng != eng or (eng != "pe" and not isps):
                    deps.add(rd)
        for r in writes:
            self.readers[r] = {}
            self.writers.setdefault(r, {})[("dma", slot) if op.is_dma else eng] = op
        for r in reads:
            self.readers.setdefault(r, {})[("dma", slot) if op.is_dma else eng] = op
        for d in deps:
            if not d.is_dma:
                d.signal = True
        op.deps = deps
        if op.is_dma:
            n = self.slots.get(slot, 0) + 1
            self.slots[slot] = n
            op.tok = (slot, 16 * n)
            self.all_dma.append(op)
        self.ops[eng].append(op)
        return op

    def barrier(self, engs=("pe", "act", "dve", "sp")):
        lasts = []
        for e in engs:
            real = [o for o in self.ops[e] if o.emit is not None and not o.is_dma]
            if real:
                lasts.append(real[-1])
        dmas = list(self.all_dma)
        self.all_dma = []
        for e in engs:
            deps = [o for o in lasts if o.eng != e] + dmas
            self.add(e, None, extra=deps)

    def emit_all(self, nc, block, sems, dsems):
        for e in self.ENG:
            n = 0
            for op in self.ops[e]:
                if op.signal:
                    n += 1
                    op.num = n
        reg = {"pe": block.tensor, "act": block.scalar, "dve": block.vector,
               "pool": block.gpsimd, "sp": block.sync}
        for e in self.ENG:
            ops = self.ops[e]
            if not ops:
                continue

            def body(eng, e=e, ops=ops):
                waited = {}
                for op in ops:
                    for d in op.deps:
                        if d.is_dma:
                            key, val, sem = ("d", d.tok[0]), d.tok[1], dsems[d.tok[0]]
                        else:
                            key, val, sem = ("e", d.eng), d.num, sems[d.eng]
                        if waited.get(key, 0) < val:
                            waited[key] = val
                            eng.wait_ge(sem, val)
                    if op.emit is None:
                        continue
                    ins = op.emit(eng)
                    if op.is_dma:
                        ins.then_inc(dsems[op.tok[0]], 16)
                    elif op.signal:
                        ins.then_inc(sems[e], 1)

            reg[e](body)


def build_program(n_pro=NBLK, n_main=NBLK, stop=99, first_halo=True):
    nc = bass.Bass("TRN2", target_bir_lowering=False)
    xm = nc.dram_tensor("xm", [TOK, D], F32, kind="ExternalInput").ap()
    xp = nc.dram_tensor("xp", [TOK, D], F32, kind="ExternalInput").ap()
    w_in = nc.dram_tensor("w_in", [D, DIN], F32, kind="ExternalInput").ap()
    w_out = nc.dram_tensor("w_out", [2 * D, D], F32, kind="ExternalInput").ap()
    w_gate = nc.dram_tensor("w_gate", [D, DFF], F32, kind="ExternalInput").ap()
    w_up = nc.dram_tensor("w_up", [D, DFF], F32, kind="ExternalInput").ap()
    w_down = nc.dram_tensor("w_down", [DFF, D], F32, kind="ExternalInput").ap()
    cst_d = nc.dram_tensor("cst", [P, NCST], F32, kind="ExternalInput").ap()
    fg_d = nc.dram_tensor("fg", [1, D], F32, kind="ExternalInput").ap()
    out = nc.dram_tensor("out", [TOK, D], F32, kind="ExternalOutput").ap()

    S = Sched()
    off = [SB_BASE]

    def alloc(name, shape, dt, at=None):
        nb = int(np.prod(shape[1:])) * (4 if dt == F32 else 2)
        nb = (nb + 31) // 32 * 32
        if at is None:
            o = off[0]
            off[0] += nb
        else:
            o = at
        assert o + nb <= SB_END, (name, o, nb)
        return nc.alloc_sbuf_tensor_at(name, shape, dt, offset=o), o + nb

    def A(name, shape, dt):
        return alloc(name, shape, dt)[0]

    cst = A("cst", [P, NCST], F32)
    identb = A("identb", [P, P], BF16)
    a_bc = A("a_bc", [P, NH], F32)
    nT = A("nT", [P, KD, TB], BF16)
    nTh = A("nTh", [P, KD, 4], BF16)
    wbuf = [A(f"wbuf{i}", [P, KD, 512], BF16) for i in range(NW)]
    wdt = A("wdt", [P, KD, 32], BF16)
    hst = A("hst", [P, D], F32)
    hbf = A("hbf", [P, D], BF16)
    hist = A("hist", [P, 32, 3], F32)
    hist_sc = A("hist_sc", [P, 16, 2], F32)
    ss = A("ss", [P, 16], F32)
    rs = A("rs", [P, 16], F32)
    ssy = A("ssy", [P, 4], F32)
    dtx = [A(f"dtx{t}", [P, NH], F32) for t in range(NT)]
    dtl = [A(f"dtl{t}", [P, NH], F32) for t in range(NT)]
    dtv = [A(f"dtv{t}", [P, NH], F32) for t in range(NT)]
    dAv = [A(f"dAv{t}", [P, NH], F32) for t in range(NT)]
    dec3 = [A(f"dec3{t}", [P, 96], F32) for t in range(NT)]
    fac = [A(f"fac{t}", [P, NH], F32) for t in range(NT)]
    bdec = A("bdec", [P, NH], F32)
    diagD = A("diagD", [P, KD, P], BF16)
    AR = off[0]
    ymixT = A("ymixT", [P, 32, TB], BF16)
    scr8k = A("scr8k", [P, D], F32)
    xnb = [A(f"xn{i}", [P, D], BF16) for i in range(2)]
    sz = []
    SZ0_OFF = off[0]
    for t in range(NT):
        if t == 2:
            SZ2_OFF = off[0]
        sz.append(A(f"sz{t}", [P, D], BF16))
    xt2, _ = alloc("xt2", [P, D], F32, at=SZ0_OFF)
    fgbc, _ = alloc("fgbc", [P, D], F32, at=SZ2_OFF)
    AR_REST = off[0]
    BT = A("BT", [P, NG, TB], BF16)
    CT = A("CT", [P, NG, TB], BF16)
    Xb = A("Xb", [P, D], BF16)
    Xd = A("Xd", [P, D], BF16)
    Btm = A("Btm", [P, NG * P], BF16)
    Mb = [A(f"Mb{i}", [P, 512], BF16) for i in range(8)]
    Lb = [A(f"Lb{i}", [P, 512], BF16) for i in range(2)]
    cbm = [A(f"cbm{i}", [P, P], BF16) for i in range(2)]
    dat = [A(f"dat{i}", [P, 512], F32) for i in range(2)]
    NF2K = 5
    f2k = [A(f"f2k{i}", [P, 516], F32) for i in range(NF2K)]
    t1 = A("t1", [P, 512], F32)
    t3 = A("t3", [P, 512], BF16)
    MIX_END = off[0]
    o = AR_REST
    hres = []
    for t in range(NT):
        h_, o = alloc(f"hres{t}", [P, D], F32, at=o)
        hres.append(h_)
    actB, o = alloc("actB", [P, KF - 32, TB], BF16, at=o)
    actA, _ = alloc("actA", [P, 32, TB], BF16, at=AR)
    sg = []
    for i in range(4):
        s_, o = alloc(f"sg{i}", [P, 512], F32, at=o)
        sg.append(s_)
    assert max(o, MIX_END) <= SB_END, (o, MIX_END)
    print("SBUF end: mixer", MIX_END, "ffn", o, "limit", SB_END)

    def actT(k):
        return actA[:, k, :] if k < 32 else actB[:, k - 32, :]

    ps = [nc.alloc_psum_tensor(f"ps{i}", [P, 512], F32) for i in range(8)]
    psb = [p_[:].bitcast(BF16) for p_ in ps]
    rot = {"A": 0, "M": 0, "f2k": 0, "w": 0, "Mb": 0, "Lb": 0, "cbm": 0, "dat": 0, "os": 0, "xn": 0}

    pools = {"A": [0, 1, 2, 3], "M": [4, 5, 6, 7]}

    def set_pools(na):
        pools["A"] = list(range(na))
        pools["M"] = list(range(na, 8))

    def bankA():
        rot["A"] = (rot["A"] + 1) % len(pools["A"])
        return pools["A"][rot["A"]]

    def bankM():
        rot["M"] = (rot["M"] + 1) % len(pools["M"])
        return pools["M"][rot["M"]]

    def nxt(name, n):
        rot[name] = (rot[name] + 1) % n
        return rot[name]

    def c1(col, n=1):
        return cst[:, col:col + n]

    def mm(out_, lhsT, rhs, start, stop, reads, writes):
        S.add("pe", lambda e: e.matmul(out_, lhsT, rhs, start=start, stop=stop), reads, writes)

    def tr(out_, in_, reads, writes):
        S.add("pe", lambda e: e.transpose(out_, in_, identb[:]), list(reads) + ["identb"], writes)

    def act(out_, in_, func, reads, writes, **kw):
        S.add("act", lambda e: e.activation(out=out_, in_=in_, func=func, **kw), reads, writes)

    def vtt(out_, in0, in1, op, reads, writes):
        S.add("dve", lambda e: e.tensor_tensor(out=out_, in0=in0, in1=in1, op=op), reads, writes)

    def vstt(out_, in0, scalar, in1, op0, op1, reads, writes):
        S.add("dve", lambda e: e.scalar_tensor_tensor(out=out_, in0=in0, scalar=scalar, in1=in1,
                                                      op0=op0, op1=op1), reads, writes)

    def vsmul(out_, in0, scalar, reads, writes):
        S.add("dve", lambda e: e.tensor_scalar_mul(out=out_, in0=in0, scalar1=scalar), reads, writes)

    def vcopy(out_, in_, reads, writes):
        S.add("dve", lambda e: e.tensor_copy(out=out_, in_=in_), reads, writes)

    def dma(q, out_, in_, slot, reads, writes):
        return S.add(q, lambda e: e.dma_start(out=out_, in_=in_), reads, writes, slot=slot)

    def wload(src, nk, ncols):
        s = nxt("w", NW)
        dma("pool", wbuf[s][:, 0:nk, 0:ncols], src, f"w{s}", [], [("w", s)])
        return s

    def wsrc(w, r0, nk, c0, ncols):
        return w[r0:r0 + nk * P, c0:c0 + ncols].rearrange("(k p) c -> p k c", p=P)

    dma("sp", cst[:], cst_d, "cst", [], ["cst"])
    vcopy(identb[:], cst[:, C_ID:C_ID + P], ["cst"], ["identb"])
    act(a_bc[:], c1(C_ALOG, NH), AF.Exp, ["cst"], ["a_bc"])
    vsmul(a_bc[:], a_bc[:], -1.0, ["a_bc"], ["a_bc"])
    for k_ in range(KD):
        vsmul(diagD[:, k_, :], cst[:, C_ID:C_ID + P], c1(C_DCH + k_), ["cst"], ["diagD"])
    S.add("dve", lambda e: e.memset(hst[:], 0.0), [], [("hst", q_) for q_ in range(4)])
    S.add("dve", lambda e: e.memset(hbf[:], 0.0), [], [("hbf", q_) for q_ in range(4)])
    S.add("dve", lambda e: e.memset(hist[:], 0.0), [], [("hist", c_) for c_ in range(32)])
    S.add("dve", lambda e: e.memset(hist_sc[:], 0.0), [], [("hsc", c_) for c_ in range(16)])

    def rstd_ops(c):
        act(rs[:, c:c + 1], ss[:, c:c + 1], AF.Ln, [("ss", c)], [("rs", c)], scale=1.0 / D, bias=EPS)
        act(rs[:, c:c + 1], rs[:, c:c + 1], AF.Exp, [("rs", c)], [("rs", c)], scale=-0.5)

    NTALL = [("nT", t) for t in range(NT)]

    def norm_stage1(src, sres, c):
        xi = nxt("xn", 2)
        xbuf, xres = xnb[xi], [("xn", xi)]
        act(xbuf[:], src, AF.Square, sres, xres + [("ss", c)], accum_out=ss[:, c:c + 1])
        rstd_ops(c)
        vsmul(xbuf[:], src, rs[:, c:c + 1], sres + [("rs", c)], xres)
        return (xbuf, xres)

    def norm_to_nT(src_fn, src_res_fn, gcol, col0, pre=None):
        bufs = dict(pre or {})

        def stage1(t):
            if t in bufs:
                return
            bufs[t] = norm_stage1(src_fn(t), src_res_fn(t), col0 + t)

        def stage2(t):
            xbuf, xres = bufs[t]
            for half in range(2):
                b = bankM()
                for kk in range(8):
                    k = half * 8 + kk
                    tr(psb[b][:, kk * P:(kk + 1) * P], xbuf[:, k * P:(k + 1) * P], xres, [("ps", b)])
                k0 = half * 8
                vtt(nT[:, k0:k0 + 8, t * P:(t + 1) * P], psb[b][:, 0:1024].rearrange("p (k c) -> p k c", k=8),
                    cst[:, gcol + k0:gcol + k0 + 8].unsqueeze(2).to_broadcast([P, 8, P]), ALU.mult,
                    [("ps", b), "cst"], [("nT", t)])

        stage1(0)
        for t in range(NT):
            if t + 1 < NT:
                stage1(t + 1)
            stage2(t)

    def formW(s, e, nk=KD):
        b = bankA()
        for k in range(nk):
            mm(ps[b][:], wbuf[s][:, k, e * P:(e + 1) * P], nT[:, k, :], k == 0, k == nk - 1,
               [("w", s)] + NTALL, [("ps", b)])
        return b

    def halo_mm(s, e, ncol):
        b = bankM()
        for k in range(KD):
            mm(ps[b][:, 0:ncol], wbuf[s][:, k, e * P:(e + 1) * P], nTh[:, k, 3 - ncol:3], k == 0, k == KD - 1,
               [("w", s), "nTh"], [("ps", b)])
        return b

    def conv_front(ch, b, halo_b):
        ui = nxt("f2k", NF2K)
        ai = nxt("f2k", NF2K)
        ue, a = f2k[ui], f2k[ai]
        ur, ar = ("f2k", ui), ("f2k", ai)
        if halo_b is not None:
            act(hist[:, ch, :], ps[halo_b][:, 0:3], AF.Copy, [("ps", halo_b), "cst"], [("hist", ch)], scale=c1(C_FLAG))
        act(ue[:, 0:3], hist[:, ch, :], AF.Copy, [("hist", ch)], [ur])
        act(ue[:, 3:515], ps[b][:], AF.Copy, [("ps", b)], [ur])
        act(a[:, 0:512], ps[b][:], AF.Identity, [("ps", b), "cst"], [ar],
            scale=c1(C_CW + ch * 4 + 3), bias=c1(C_CB + ch))
        act(hist[:, ch, :], ue[:, 512:515], AF.Copy, [ur], [("hist", ch)])
        for tap in (2, 1, 0):
            vstt(a[:, 0:512], ue[:, tap:tap + 512], c1(C_CW + ch * 4 + tap), a[:, 0:512], ALU.mult, ALU.add,
                 [ur, ar, "cst"], [ar])
        return ai

    def conv_back(dst, dres, ai):
        act(dst, f2k[ai][:, 0:512], AF.Silu, [("f2k", ai)], dres)

    yn_idx = {}
    FGK = [("sz", t_, j_) for t_ in (2, 3) for j_ in range(4)]
    XT2K = [("sz", t_, j_) for t_ in (0, 1) for j_ in range(4)]

    prefetched = set()
    hoisted = {}
    deferred = []

    def block(bi, xsrc, main, first_main, out_rows, barrier_after_z, nxt_blk):
        def xsrc_fn(t):
            buf, keys = (scr8k, ["scr8k"]) if t % 2 == 0 else (xt2, XT2K)
            if (bi, main, t) not in prefetched:
                dma("sp", buf[:], xsrc[(bi * NT + t) * P:(bi * NT + t + 1) * P, :], f"xt{t % 2}", [], keys)
            return buf[:]

        def prefetch_next():
            if nxt_blk is None:
                return
            nbi, nmain, nsrc = nxt_blk
            for t in range(2):
                buf, keys = (scr8k, ["scr8k"]) if t % 2 == 0 else (xt2, XT2K)
                dma("sp", buf[:], nsrc[(nbi * NT + t) * P:(nbi * NT + t + 1) * P, :], f"xt{t % 2}", [], keys)
                prefetched.add((nbi, nmain, t))

        def hoist_next_norm():
            if nxt_blk is None:
                return
            nbi, nmain, nsrc = nxt_blk
            d_ = {}
            for t in range(2):
                buf, keys = (scr8k, ["scr8k"]) if t % 2 == 0 else (xt2, XT2K)
                d_[t] = norm_stage1(buf[:], keys, t)
            hoisted[(nbi, nmain)] = d_
            for t in (2, 3):
                buf, keys = (scr8k, ["scr8k"]) if t % 2 == 0 else (xt2, XT2K)
                dma("sp", buf[:], nsrc[(nbi * NT + t) * P:(nbi * NT + t + 1) * P, :], f"xt{t % 2}", [], keys)
                prefetched.add((nbi, nmain, t))

        norm_to_nT(xsrc_fn, lambda t: ["scr8k"] if t % 2 == 0 else XT2K, C_GMIX, 0, hoisted.pop((bi, main), None))
        while deferred:
            deferred.pop(0)()

        dma("pool", wdt[:], wsrc(w_in, 0, KD, OFF_DT, 32), "wdt", [], ["wdt"])
        set_pools(3)

        def dt_part1():
            dtb_ = []
            for t in range(NT):
                b = bankM()
                for k in range(KD):
                    mm(ps[b][:, 0:NH], nT[:, k, t * P:(t + 1) * P], wdt[:, k, :], k == 0, k == KD - 1,
                       ["wdt", ("nT", t)], [("ps", b)])
                dtb_.append(b)
            for t in range(NT):
                b = dtb_[t]
                vtt(dtx[t][:], ps[b][:, 0:NH], c1(C_DTB, NH), ALU.add, [("ps", b), "cst"], [("dtx", t)])
                vstt(dtl[t][:], dtx[t][:], -1.0, dtx[t][:], ALU.mult, ALU.max, [("dtx", t)], [("dtl", t)])
                act(dtl[t][:], dtl[t][:], AF.Exp, [("dtl", t)], [("dtl", t)], scale=-1.0)
                act(dtl[t][:], dtl[t][:], AF.Ln, [("dtl", t)], [("dtl", t)], bias=1.0)
                vstt(dtv[t][:], dtx[t][:], 0.0, dtl[t][:], ALU.max, ALU.add, [("dtx", t), ("dtl", t)], [("dtv", t)])
                vtt(dAv[t][:], dtv[t][:], a_bc[:], ALU.mult, [("dtv", t), "a_bc"], [("dAv", t)])

        if not main:
            dt_part1()

        def dt_part2():
            for t in range(NT):
                b2 = bankM()
                for i, mcol in enumerate((C_MGT, C_MLE, C_ONE)):
                    mm(ps[b2][:, i * NH:(i + 1) * NH], cst[:, mcol:mcol + P], dAv[t][:], True, True,
                       ["cst", ("dAv", t)], [("ps", b2)])
                act(dec3[t][:], ps[b2][:, 0:96], AF.Exp, [("ps", b2)], [("dec3", t)])

        if main:
            for j in range(4):
                s = wload(wsrc(w_in, 0, KD, OFF_Z + j * 512, 512), KD, 512)
                for t in range(NT):
                    b = bankA()
                    for k in range(KD):
                        mm(ps[b][:], nT[:, k, t * P:(t + 1) * P], wbuf[s][:, k, :], k == 0, k == KD - 1,
                           [("w", s), ("nT", t)], [("ps", b)])
                    act(sz[t][:, j * 512:(j + 1) * 512], ps[b][:], AF.Silu, [("ps", b)], [("sz", t, j)])
                if j == 0:
                    dt_part1()
            dt_part2()
        if barrier_after_z:
            S.barrier()
        if main and stop <= 1:
            return
        secs = [(OFF_X, 4, 0), (OFF_B, 2, 16)] + ([(OFF_C, 2, 24)] if main else [])
        pending = None
        for c0, nb, ch0 in secs:
            for jb in range(nb):
                s = wload(wsrc(w_in, 0, KD, c0 + jb * 512, 512), KD, 512)
                for e in range(4):
                    ch = ch0 + jb * 4 + e
                    b = formW(s, e)
                    hb = halo_mm(s, e, 3) if first_main else None
                    if ch < 16:
                        dst, dres = ymixT[:, ch, :], [("ymT", ch, tt) for tt in range(NT)]
                    elif ch < 24:
                        dst, dres = BT[:, ch - 16, :], [("BT", ch - 16)]
                    else:
                        dst, dres = CT[:, ch - 24, :], [("CT", ch - 24)]
                    ai = conv_front(ch, b, hb)
                    if pending is not None:
                        conv_back(*pending)
                    pending = (dst, dres, ai)
                if (not main) and c0 == OFF_X and jb == 0:
                    dt_part2()
        conv_back(*pending)
        if main and stop <= 2:
            return

        def ssd1_pre(t):
            tc = slice(t * P, (t + 1) * P)
            for half in range(2):
                b = bankM()
                for kk in range(8):
                    k = half * 8 + kk
                    tr(psb[b][:, kk * P:(kk + 1) * P], ymixT[:, k, tc], [("ymT", k, t)], [("ps", b)])
                hs = slice(half * 1024, (half + 1) * 1024)
                dtb = dtv[t][:, half * 16:(half + 1) * 16].unsqueeze(2).to_broadcast([P, 16, HD])
                vtt(Xb[:, hs].rearrange("p (h d) -> p h d", h=16),
                    psb[b][:, 0:1024].rearrange("p (h d) -> p h d", h=16), dtb, ALU.mult,
                    [("ps", b), ("dtv", t)], [("Xb", half)])
                vtt(Xd[:, hs].rearrange("p (h d) -> p h d", h=16),
                    Xb[:, hs].rearrange("p (h d) -> p h d", h=16),
                    dec3[t][:, half * 16:(half + 1) * 16].unsqueeze(2).to_broadcast([P, 16, HD]), ALU.mult,
                    [("Xb", half), ("dec3", t)], [("Xd", half)])
                yield
            b = bankM()
            for g in range(NG):
                tr(psb[b][:, g * P:(g + 1) * P], BT[:, g, tc], [("BT", g)], [("ps", b)])
            act(Btm[:], psb[b][:, 0:1024], AF.Copy, [("ps", b)], ["Btm"])
            yield

        def state_update(t):
            for q in range(4):
                qs = slice(q * 512, (q + 1) * 512)
                b = bankM()
                for g2 in range(2):
                    g = q * 2 + g2
                    mm(ps[b][:, g2 * 256:(g2 + 1) * 256], Btm[:, g * P:(g + 1) * P], Xd[:, g * 256:(g + 1) * 256],
                       True, True, ["Btm", ("Xd", q // 2)], [("ps", b)])
                vtt(hst[:, qs].rearrange("p (h d) -> p h d", h=8), hst[:, qs].rearrange("p (h d) -> p h d", h=8),
                    dec3[t][:, 64 + q * 8:64 + q * 8 + 8].unsqueeze(2).to_broadcast([P, 8, HD]), ALU.mult,
                    [("hst", q), ("dec3", t)], [("hst", q)])
                vtt(hst[:, qs], hst[:, qs], ps[b][:], ALU.add, [("hst", q), ("ps", b)], [("hst", q)])
                act(hbf[:, qs], hst[:, qs], AF.Copy, [("hst", q)], [("hbf", q)])
                yield

        def stageA(t):
            tc = slice(t * P, (t + 1) * P)

            def mk_dat(g, tt=None):
                tt = t if tt is None else tt
                di = g % 2
                for hh in range(4):
                    act(dat[di][:, hh * P:(hh + 1) * P], cst[:, C_MLE:C_MLE + P], AF.Copy,
                        ["cst", ("dAv", tt)], [("dat", di)], scale=dAv[tt][:, 4 * g + hh:4 * g + hh + 1])

            def group(g):
                di = li = ci = g % 2
                b = bankM()
                mm(ps[b][:], cst[:, C_MGT:C_MGT + P], dat[di][:], True, True, ["cst", ("dat", di)], [("ps", b)])
                act(Lb[li][:], ps[b][:], AF.Exp, [("ps", b)], [("Lb", li)])
                b2 = bankM()
                mm(ps[b2][:, 0:P], BT[:, g, tc], CT[:, g, tc], True, True, [("BT", g), ("CT", g)], [("ps", b2)])
                if g + 2 < NG:
                    mk_dat(g + 2)
                vtt(cbm[ci][:], ps[b2][:, 0:P], cst[:, C_MLE:C_MLE + P], ALU.mult, [("ps", b2), "cst"], [("cbm", ci)])
                vtt(Mb[g][:].rearrange("p (h l) -> p h l", h=4),
                    Lb[li][:].rearrange("p (h l) -> p h l", h=4),
                    cbm[ci][:].unsqueeze(1).to_broadcast([P, 4, P]), ALU.mult,
                    [("Lb", li), ("cbm", ci)], [("Mb", g)])

            if t == 0:
                mk_dat(0, t)
                mk_dat(1, t)
            pre = ssd1_pre(t)
            for g in range(NG):
                group(g)
                yield
                if g % 2 == 1:
                    next(pre, None)
                    yield
            for _ in pre:
                yield
            if t + 1 < NT:
                mk_dat(0, t + 1)
                mk_dat(1, t + 1)

        def stageB(t):
            tc = slice(t * P, (t + 1) * P)
            for q in range(4):
                half = q // 2
                qs = slice(q * 512, (q + 1) * 512)
                bd = bankM()
                for j in range(4):
                    k = q * 4 + j
                    S.add("pe", lambda e, bd=bd, j=j, k=k: e.matmul(
                        ps[bd][:, j * P:(j + 1) * P], ymixT[:, k, tc], diagD[:, k, :], start=(j == 0), stop=False,
                        skip_group_check=True), [("ymT", k, t), "diagD"], [("ps", bd)])
                for hh in range(8):
                    h = q * 8 + hh
                    g = h // 4
                    S.add("pe", lambda e, bd=bd, hh=hh, h=h, g=g: e.matmul(
                        ps[bd][:, hh * HD:(hh + 1) * HD], Mb[g][:, (h % 4) * P:(h % 4 + 1) * P],
                        Xb[:, h * HD:(h + 1) * HD], start=False, stop=(hh == 7), skip_group_check=True),
                        [("Mb", g), ("Xb", half)], [("ps", bd)])
                bo = bankM()
                for g2 in range(2):
                    g = q * 2 + g2
                    mm(ps[bo][:, g2 * 256:(g2 + 1) * 256], CT[:, g, tc], hbf[:, g * 256:(g + 1) * 256], True, True,
                       [("CT", g), ("hbf", q)], [("ps", bo)])
                vtt(t1[:].rearrange("p (h d) -> p h d", h=8), ps[bo][:].rearrange("p (h d) -> p h d", h=8),
                    dec3[t][:, 32 + q * 8:32 + q * 8 + 8].unsqueeze(2).to_broadcast([P, 8, HD]), ALU.mult,
                    [("ps", bo), ("dec3", t)], ["t1"])
                vtt(t1[:], ps[bd][:], t1[:], ALU.add, [("ps", bd), "t1"], ["t1"])
                vtt(scr8k[:, qs], t1[:], sz[t][:, qs], ALU.mult, ["t1", ("sz", t, q)], ["scr8k"])
                act(t3[:], scr8k[:, qs], AF.Square, ["scr8k"], ["t3", ("ssy", q)], accum_out=ssy[:, q:q + 1])
                yield
            c = 8 + t
            S.add("dve", lambda e, c=c: e.reduce_sum(out=ss[:, c:c + 1], in_=ssy[:, 0:4], axis=AX.X),
                  [("ssy", q) for q in range(4)], [("ss", c)])
            rstd_ops(c)
            yi = nxt("xn", 2)
            yn_idx[t] = yi
            act(xnb[yi][:], scr8k[:], AF.Copy, ["scr8k", ("rs", c)], [("xn", yi)], scale=rs[:, c:c + 1])
            yield
            yield from state_update(t)

        def y_transposes(t):
            tc = slice(t * P, (t + 1) * P)
            yi = yn_idx[t]
            for half in range(2):
                b = bankM()
                for kk in range(8):
                    k = half * 8 + kk
                    tr(psb[b][:, kk * P:(kk + 1) * P], xnb[yi][:, k * P:(k + 1) * P], [("xn", yi)], [("ps", b)])
                k0 = half * 8
                vtt(ymixT[:, k0:k0 + 8, tc], psb[b][:, 0:1024].rearrange("p (k c) -> p k c", k=8),
                    cst[:, C_GSSM + k0:C_GSSM + k0 + 8].unsqueeze(2).to_broadcast([P, 8, P]), ALU.mult,
                    [("ps", b), "cst"], [("ymT", k0 + kk, t) for kk in range(8)])
                yield

        def sc_group(j):
            s = wload(wsrc(w_in, 0, KD, OFF_CC + j * 512, 512), KD, 512)
            ccs = []
            for e in range(4):
                b = formW(s, e)
                hb = halo_mm(s, e, 2) if first_main else None
                fi = nxt("f2k", NF2K)
                act(f2k[fi][:, 0:512], ps[b][:], AF.Copy, [("ps", b)], [("f2k", fi)])
                ccs.append(fi)
                if hb is not None:
                    act(hist_sc[:, j * 4 + e, :], ps[hb][:, 0:2], AF.Copy, [("ps", hb), "cst"], [("hsc", j * 4 + e)],
                        scale=c1(C_FLAG))
                yield
            s = wload(wsrc(w_in, 0, KD, OFF_CX + j * 512, 512), KD, 512)
            for e in range(4):
                ch = j * 4 + e
                b = formW(s, e)
                if first_main:
                    hb = halo_mm(s, e, 2)
                    vtt(hist_sc[:, ch, :], hist_sc[:, ch, :], ps[hb][:, 0:2], ALU.mult, [("hsc", ch), ("ps", hb)], [("hsc", ch)])
                gi = nxt("f2k", NF2K)
                while gi in ccs:
                    gi = nxt("f2k", NF2K)
                ge = f2k[gi]
                a = f2k[ccs[e]]
                vcopy(ge[:, 0:2], hist_sc[:, ch, :], [("hsc", ch)], [("f2k", gi)])
                vtt(ge[:, 2:514], ps[b][:], a[:, 0:512], ALU.mult, [("ps", b), ("f2k", ccs[e])], [("f2k", gi)])
                act(a[:, 0:512], ge[:, 2:514], AF.Copy, [("f2k", gi), "cst"], [("f2k", ccs[e])], scale=c1(C_SCW + ch * 3 + 2))
                vcopy(hist_sc[:, ch, :], ge[:, 512:514], [("f2k", gi)], [("hsc", ch)])
                for tap in (1, 0):
                    vstt(a[:, 0:512], ge[:, tap:tap + 512], c1(C_SCW + ch * 3 + tap), a[:, 0:512], ALU.mult, ALU.add,
                         [("f2k", gi), ("f2k", ccs[e]), "cst"], [("f2k", ccs[e])])
                yield
            s = wload(wsrc(w_in, 0, KD, OFF_CB + j * 512, 512), KD, 512)
            for e in range(4):
                ch = j * 4 + e
                b = formW(s, e)
                vtt(ymixT[:, 16 + ch, :], ps[b][:], f2k[ccs[e]][:, 0:512], ALU.mult, [("ps", b), ("f2k", ccs[e])],
                    [("ymT", 16 + ch, tt) for tt in range(NT)])
                yield

        def run(gen):
            for _ in gen:
                pass

        if not main:
            set_pools(4)
            accb = [bankA() for _ in range(4)]
            for t in range(NT):
                vtt(fac[t][:], dtv[t][:], dec3[t][:, 0:NH], ALU.mult, [("dtv", t), ("dec3", t)], [("fac", t)])
                for t2 in range(t + 1, NT):
                    vtt(fac[t][:], fac[t][:], dec3[t2][:, 64:96], ALU.mult, [("fac", t), ("dec3", t2)], [("fac", t)])
            vtt(bdec[:], dec3[0][:, 64:96], dec3[1][:, 64:96], ALU.mult, [("dec3", 0), ("dec3", 1)], ["bdec"])
            for t2 in (2, 3):
                vtt(bdec[:], bdec[:], dec3[t2][:, 64:96], ALU.mult, ["bdec", ("dec3", t2)], ["bdec"])
            for t in range(NT):
                tc = slice(t * P, (t + 1) * P)
                for half in range(2):
                    b = bankM()
                    for kk in range(8):
                        k = half * 8 + kk
                        tr(psb[b][:, kk * P:(kk + 1) * P], ymixT[:, k, tc], [("ymT", k, t)], [("ps", b)])
                    hs = slice(half * 1024, (half + 1) * 1024)
                    vtt(Xd[:, hs].rearrange("p (h d) -> p h d", h=16),
                        psb[b][:, 0:1024].rearrange("p (h d) -> p h d", h=16),
                        fac[t][:, half * 16:(half + 1) * 16].unsqueeze(2).to_broadcast([P, 16, HD]), ALU.mult,
                        [("ps", b), ("fac", t)], [("Xd", half)])
                b = bankM()
                for g in range(NG):
                    tr(psb[b][:, g * P:(g + 1) * P], BT[:, g, tc], [("BT", g)], [("ps", b)])
                act(Btm[:], psb[b][:, 0:1024], AF.Copy, [("ps", b)], ["Btm"])
                for q in range(4):
                    for g2 in range(2):
                        g = q * 2 + g2
                        S.add("pe", lambda e, q=q, g2=g2, g=g, t=t: e.matmul(
                            ps[accb[q]][:, g2 * 256:(g2 + 1) * 256], Btm[:, g * P:(g + 1) * P],
                            Xd[:, g * 256:(g + 1) * 256], start=(t == 0 and g2 == 0),
                            stop=(t == NT - 1 and g2 == 1), skip_group_check=True),
                            ["Btm", ("Xd", q // 2)], [("ps", accb[q])])
                if t == 0:
                    prefetch_next()
                if t == 2:
                    hoist_next_norm()
            for q in range(4):
                qs = slice(q * 512, (q + 1) * 512)
                vtt(hst[:, qs].rearrange("p (h d) -> p h d", h=8), hst[:, qs].rearrange("p (h d) -> p h d", h=8),
                    bdec[:, q * 8:q * 8 + 8].unsqueeze(2).to_broadcast([P, 8, HD]), ALU.mult,
                    [("hst", q), "bdec"], [("hst", q)])
                vtt(hst[:, qs], hst[:, qs], ps[accb[q]][:], ALU.add, [("hst", q), ("ps", accb[q])], [("hst", q)])
            return

        def micro_steps():
            yield from stageA(0)
            for t in range(NT):
                yield from stageB(t)
                if t + 1 < NT:
                    yield from stageA(t + 1)
                if t > 0:
                    yield from y_transposes(t - 1)
            yield from y_transposes(NT - 2) if False else iter(())

        def sc_units():
            for j in range(4):
                yield from sc_group(j)

        micro = micro_steps()
        MICRO_PER_UNIT = 2
        for _ in sc_units():
            for _i in range(MICRO_PER_UNIT):
                next(micro, None)
        run(micro)
        if stop <= 5:
            return

        def outproj_part(j, kh, banks, first, last):
            s = wload(wsrc(w_out, kh * D, KD, j * 512, 512), KD, 512)
            for t in range(NT):
                for k in range(KD):
                    kg = kh * KD + k
                    mm(ps[banks[t]][:], ymixT[:, kg, t * P:(t + 1) * P], wbuf[s][:, k, :],
                       first and k == 0, last and k == KD - 1, [("w", s), ("ymT", kg, t)], [("ps", banks[t])])

        set_pools(4)
        banks = [bankA() for _ in range(NT)]
        outproj_part(0, 1, banks, True, False)
        run(y_transposes(NT - 1))
        S.barrier()
        for t in range(NT):
            dma("sp", hres[t][:], xsrc[(bi * NT + t) * P:(bi * NT + t + 1) * P, :], f"xr{t}", [], [("hres", t)])
        for j in range(4):
            if j > 0:
                banks = [bankA() for _ in range(NT)]
                outproj_part(j, 1, banks, True, False)
            outproj_part(j, 0, banks, False, True)
            for t in range(NT):
                vtt(hres[t][:, j * 512:(j + 1) * 512], hres[t][:, j * 512:(j + 1) * 512], ps[banks[t]][:], ALU.add,
                    [("hres", t), ("ps", banks[t])], [("hres", t)])
        S.barrier()
        if stop <= 6:
            return

        prefetch_next()
        dma("sp", fgbc[:], fg_d.partition_broadcast(P), "fg", [], FGK)
        norm_to_nT(lambda t: hres[t][:], lambda t: [("hres", t)], C_GFFN, 4)
        for j in range(KF // 4):
            s = wload(wsrc(w_gate, 0, KD, j * 512, 512), KD, 512)
            for e in range(4):
                b = formW(s, e)
                act(sg[e][:], ps[b][:], AF.Silu, [("ps", b)], [("sg", e)])
            s = wload(wsrc(w_up, 0, KD, j * 512, 512), KD, 512)
            for e in range(4):
                b = formW(s, e)
                vtt(actT(j * 4 + e), ps[b][:], sg[e][:], ALU.mult, [("ps", b), ("sg", e)], [("actT", j * 4 + e)])
        if stop <= 7:
            return
        hoist_next_norm()
        for j in range(4):
            banks = [bankA() for _ in range(NT)]
            for (k0, nk) in ((0, 16), (16, 16), (32, 12)):
                s = wload(wsrc(w_down, k0 * P, nk, j * 512, 512), nk, 512)
                for t in range(NT):
                    for k in range(nk):
                        kg = k0 + k
                        mm(ps[banks[t]][:], actT(kg)[:, t * P:(t + 1) * P], wbuf[s][:, k, :], kg == 0, kg == KF - 1,
                           [("w", s), ("actT", kg)], [("ps", banks[t])])
            for t in range(NT):
                vtt(hres[t][:, j * 512:(j + 1) * 512], hres[t][:, j * 512:(j + 1) * 512], ps[banks[t]][:], ALU.add,
                    [("hres", t), ("ps", banks[t])], [("hres", t)])
        def final_norm():
            for t in range(NT):
                c = 12 + t
                xi = nxt("xn", 2)
                act(xnb[xi][:], hres[t][:], AF.Square, [("hres", t)], [("xn", xi), ("ss", c)], accum_out=ss[:, c:c + 1])
                rstd_ops(c)
                vstt(hres[t][:], hres[t][:], rs[:, c:c + 1], fgbc[:], ALU.mult, ALU.mult,
                     [("hres", t), ("rs", c)] + FGK, [("hres", t)])
                stores.append(dma("sp", out[(out_rows + t) * P:(out_rows + t + 1) * P, :], hres[t][:], f"st{t}",
                                  [("hres", t)], []))

        deferred.append(final_norm)

    stores = []
    for pb in range(n_pro):
        nb_ = (pb + 1, False, xp) if pb + 1 < n_pro else ((0, True, xm) if n_main > 0 else None)
        block(pb, xp, False, False, None, False, nb_)
    for q in range(4):
        qs = slice(q * 512, (q + 1) * 512)
        vsmul(hst[:, qs], hst[:, qs], c1(C_FLAG), [("hst", q), "cst"], [("hst", q)])
        act(hbf[:, qs], hst[:, qs], AF.Copy, [("hst", q)], [("hbf", q)])
    vcopy(nTh[:, :, 0:3], nT[:, :, TB - 3:TB], NTALL, ["nTh"])
    for bi in range(n_main):
        block(bi, xm, True, bi == 0 and first_halo, bi * NT, bi > 0, (bi + 1, True, xm) if bi + 1 < n_main else None)
    while deferred:
        deferred.pop(0)()
    S.add("sp", None, extra=stores)

    import contextlib
    with contextlib.ExitStack() as es:
        sems = {e: es.enter_context(nc.semaphore(f"sem_{e}")) for e in Sched.ENG}
        dsems = {sl: es.enter_context(nc.semaphore(f"dsem_{sl}")) for sl in S.slots}
        blk = es.enter_context(nc.Block())
        S.emit_all(nc, blk, sems, dsems)
    return nc


_CACHE = {}


def _consts(inp, flag):
    c = np.zeros((P, NCST), np.float32)
    k = np.arange(P)
    c[:, C_MGT:C_MGT + P] = (k[:, None] > k[None, :]).astype(np.float32)
    c[:, C_MLE:C_MLE + P] = (k[:, None] <= k[None, :]).astype(np.float32)
    c[:, C_ONE:C_ONE + P] = 1.0
    c[:, C_GMIX:C_GMIX + KD] = inp["norm_mix_g"][0].reshape(KD, P).T
    c[:, C_GFFN:C_GFFN + KD] = inp["norm_ffn_g"][0].reshape(KD, P).T
    c[:, C_GSSM:C_GSSM + KD] = inp["ssm_norm_g"][0].reshape(KD, P).T
    c[:, C_CW:C_CW + 128] = inp["ssm_conv_w"][0].reshape(4, 32, P).transpose(2, 1, 0).reshape(P, 128)
    c[:, C_CB:C_CB + 32] = inp["ssm_conv_b"][0].reshape(32, P).T
    c[:, C_SCW:C_SCW + 48] = inp["sc_conv_w"][0].reshape(3, 16, P).transpose(2, 1, 0).reshape(P, 48)
    c[:, C_DTB:C_DTB + NH] = inp["ssm_dt_bias"][0][None, :]
    c[:, C_ALOG:C_ALOG + NH] = inp["ssm_A_log"][0][None, :]
    c[:, C_DSK:C_DSK + NH] = inp["ssm_D"][0][None, :]
    c[:, C_FLAG] = flag
    c[:, C_ID:C_ID + P] = np.eye(P, dtype=np.float32)
    c[:, C_DCH:C_DCH + KD] = np.repeat(inp["ssm_D"][0], HD).reshape(KD, P).T
    return c


def kernel(**inputs):
    inp = {k: np.asarray(v, dtype=np.float32) for k, v in inputs.items()}
    x = inp["x"]
    if "nc" not in _CACHE:
        _CACHE["nc"] = build_program()
    nc = _CACHE["nc"]
    shared = {
        "w_in": np.ascontiguousarray(inp["w_in"][0]),
        "w_out": np.ascontiguousarray(inp["w_out"][0]),
        "w_gate": np.ascontiguousarray(inp["w_gate"][0]),
        "w_up": np.ascontiguousarray(inp["w_up"][0]),
        "w_down": np.ascontiguousarray(inp["w_down"][0]),
        "fg": np.ascontiguousarray(inp["norm_final_g"].reshape(1, D)),
    }
    in_maps = []
    for c in range(8):
        b, half = c // 2, c % 2
        m = dict(shared)
        m["xm"] = np.ascontiguousarray(x[b, half * TOK:(half + 1) * TOK])
        m["xp"] = np.ascontiguousarray(x[b, (1 - half) * TOK:(2 - half) * TOK])
        m["cst"] = _consts(inp, float(half))
        in_maps.append(m)
    res = run_bass_kernel_spmd(nc, in_maps, core_ids=list(range(8)))
    outp = np.empty((4, 2 * TOK, D), np.float32)
    for c in range(8):
        b, half = c // 2, c % 2
        outp[b, half * TOK:(half + 1) * TOK] = res.results[c]["out"]
    return outp
```
